# Optimizing a Trainium2 kernel written in Bass

```python
import math
import jax, jax.numpy as jnp
from jax import lax
import numpy as np

D_MODEL = 1024
BATCH = 8
SEQ = 4096
DEPTH = 4
DEC_BATCH = 2
DEC_SEQ = 8192
PAST_LEN = 128

N_MIXERS = 2
N_A_LAYERS = (DEPTH + 1) // 2
N_B_LAYERS = DEPTH // 2

A_HEADS = 8
A_Q_LORA = 384
A_KV_LORA = 256
A_NOPE = 128
A_ROPE = 64
A_V = 128
A_WIDTH = A_HEADS * A_V
A_IN = A_Q_LORA + A_KV_LORA + A_ROPE + A_WIDTH
ROPE_THETA = 10000.0
Q_BLOCK = 128

B_WIDTH = 2 * D_MODEL
B_GROUPS = 8
B_GROUP_DIM = B_WIDTH // B_GROUPS
B_CHUNK = 128
B_IN = 3 * B_WIDTH

RMS_EPS = 1e-6
LN_EPS = 1e-5

kernel_name = "hybrid_mla_gmlp_encoder"


def rmsnorm(x, g):
    x32 = x.astype(jnp.float32)
    y = x32 * lax.rsqrt(jnp.mean(x32 * x32, axis=-1, keepdims=True) + RMS_EPS)
    return y.astype(x.dtype) * g


def layernorm(x, g, b):
    x32 = x.astype(jnp.float32)
    mu = jnp.mean(x32, axis=-1, keepdims=True)
    xc = x32 - mu
    y = xc * lax.rsqrt(jnp.mean(xc * xc, axis=-1, keepdims=True) + LN_EPS)
    return y.astype(x.dtype) * g + b


def rope_tables(seq, dtype):
    inv = ROPE_THETA ** (-jnp.arange(0, A_ROPE, 2, dtype=jnp.float32) / A_ROPE)
    ang = jnp.arange(seq, dtype=jnp.float32)[:, None] * inv[None, :]
    ang = jnp.concatenate([ang, ang], axis=-1)
    return jnp.cos(ang).astype(dtype), jnp.sin(ang).astype(dtype)


def apply_rope(x, cos, sin):
    half = A_ROPE // 2
    x1, x2 = x[..., :half], x[..., half:]
    rot = jnp.concatenate([-x2, x1], axis=-1)
    return x * cos + rot * sin


def mla_attention(q_nope, q_rope, k_nope, k_rope, v):
    bsz, seq = q_nope.shape[0], q_nope.shape[1]
    nb = seq // Q_BLOCK
    scale = (A_NOPE + A_ROPE) ** -0.5

    def blocks(t):
        return jnp.moveaxis(t.reshape(bsz, nb, Q_BLOCK, *t.shape[2:]), 1, 0)

    def one_block(qb):
        qn, qr = qb
        s = (jnp.einsum('bqhd,bkhd->bhqk', qn, k_nope)
             + jnp.einsum('bqhr,bkr->bhqk', qr, k_rope))
        p = jax.nn.softmax(s.astype(jnp.float32) * scale, axis=-1).astype(v.dtype)
        return jnp.einsum('bhqk,bkhd->bqhd', p, v)

    o = lax.map(one_block, (blocks(q_nope), blocks(q_rope)))
    return jnp.moveaxis(o, 0, 1).reshape(bsz, seq, A_WIDTH)


def mla_layer(x, g, w_in, q_norm, kv_norm, w_q_up, w_kv_up, w_out):
    bsz, seq, _ = x.shape
    h = rmsnorm(x, g)
    proj = h @ w_in
    q_lat, kv_lat, k_rope, gate = jnp.split(
        proj, [A_Q_LORA, A_Q_LORA + A_KV_LORA, A_Q_LORA + A_KV_LORA + A_ROPE], axis=-1)
    q = (rmsnorm(q_lat, q_norm) @ w_q_up).reshape(bsz, seq, A_HEADS, A_NOPE + A_ROPE)
    kv = (rmsnorm(kv_lat, kv_norm) @ w_kv_up).reshape(bsz, seq, A_HEADS, A_NOPE + A_V)
    q_nope, q_rope = q[..., :A_NOPE], q[..., A_NOPE:]
    k_nope, v = kv[..., :A_NOPE], kv[..., A_NOPE:]
    cos, sin = rope_tables(seq, x.dtype)
    q_rope = apply_rope(q_rope, cos[:, None, :], sin[:, None, :])
    k_rope = apply_rope(k_rope, cos, sin)
    o = mla_attention(q_nope, q_rope, k_nope, k_rope, v)
    return x + (o * jax.nn.silu(gate)) @ w_out


def gmlp_layer(x, g, w_in, ln_g, ln_b, w_s, b_s, w_out):
    bsz, seq, _ = x.shape
    nc = seq // B_CHUNK
    h = rmsnorm(x, g)
    u, v, gate = jnp.split(h @ w_in, 3, axis=-1)
    u = jax.nn.gelu(u, approximate=False)
    v = layernorm(jax.nn.gelu(v, approximate=False), ln_g, ln_b)
    vc = v.reshape(bsz, nc, B_CHUNK, B_GROUPS, B_GROUP_DIM)
    sv = jnp.einsum('gpq,bnqgd->bnpgd', w_s, vc) + jnp.transpose(b_s)[:, :, None]
    s = u * sv.reshape(bsz, seq, B_WIDTH)
    return x + (s * jax.nn.silu(gate)) @ w_out


def trunk(x, norm_g, final_g,
          a_w_in, a_q_norm, a_kv_norm, a_w_q_up, a_w_kv_up, a_w_out,
          b_w_in, b_ln_g, b_ln_b, b_w_s, b_b_s, b_w_out):
    for i in range(DEPTH):
        j = i // N_MIXERS
        if i % N_MIXERS == 0:
            x = mla_layer(x, norm_g[i], a_w_in[j], a_q_norm[j], a_kv_norm[j],
                          a_w_q_up[j], a_w_kv_up[j], a_w_out[j])
        else:
            x = gmlp_layer(x, norm_g[i], b_w_in[j], b_ln_g[j], b_ln_b[j],
                           b_w_s[j], b_b_s[j], b_w_out[j])
    return rmsnorm(x, final_g)


def setup_inputs(seed: int = 0) -> dict:
    key = jax.random.key(seed)
    ks = jax.random.split(key, 20)
    f32 = jnp.float32

    def nrm(k, shape, scale):
        return jax.random.normal(k, shape, f32) * scale

    return {
        "x_prompt": nrm(ks[0], (BATCH, SEQ, D_MODEL), 1.0),
        "x_sample": nrm(ks[1], (DEC_BATCH, DEC_SEQ, D_MODEL), 1.0),
        "norm_g": 1.0 + nrm(ks[2], (DEPTH, D_MODEL), 0.02),
        "final_g": 1.0 + nrm(ks[3], (D_MODEL,), 0.02),
        "a_w_in": nrm(ks[4], (N_A_LAYERS, D_MODEL, A_IN), D_MODEL ** -0.5),
        "a_q_norm": 1.0 + nrm(ks[5], (N_A_LAYERS, A_Q_LORA), 0.02),
        "a_kv_norm": 1.0 + nrm(ks[6], (N_A_LAYERS, A_KV_LORA), 0.02),
        "a_w_q_up": nrm(ks[7], (N_A_LAYERS, A_Q_LORA, A_HEADS * (A_NOPE + A_ROPE)), A_Q_LORA ** -0.5),
        "a_w_kv_up": nrm(ks[8], (N_A_LAYERS, A_KV_LORA, A_HEADS * (A_NOPE + A_V)), A_KV_LORA ** -0.5),
        "a_w_out": nrm(ks[9], (N_A_LAYERS, A_WIDTH, D_MODEL), A_WIDTH ** -0.5),
        "b_w_in": nrm(ks[10], (N_B_LAYERS, D_MODEL, B_IN), D_MODEL ** -0.5),
        "b_ln_g": 1.0 + nrm(ks[11], (N_B_LAYERS, B_WIDTH), 0.02),
        "b_ln_b": nrm(ks[12], (N_B_LAYERS, B_WIDTH), 0.02),
        "b_w_s": nrm(ks[13], (N_B_LAYERS, B_GROUPS, B_CHUNK, B_CHUNK), B_CHUNK ** -0.5),
        "b_b_s": 1.0 + nrm(ks[14], (N_B_LAYERS, B_GROUPS, B_CHUNK), 0.02),
        "b_w_out": nrm(ks[15], (N_B_LAYERS, B_WIDTH, D_MODEL), B_WIDTH ** -0.5),
    }


def reference(x_prompt, x_sample, norm_g, final_g,
              a_w_in, a_q_norm, a_kv_norm, a_w_q_up, a_w_kv_up, a_w_out,
              b_w_in, b_ln_g, b_ln_b, b_w_s, b_b_s, b_w_out):
    y_prompt = trunk(x_prompt, norm_g, final_g,
                     a_w_in, a_q_norm, a_kv_norm, a_w_q_up, a_w_kv_up, a_w_out,
                     b_w_in, b_ln_g, b_ln_b, b_w_s, b_b_s, b_w_out)
    y_sample = trunk(x_sample, norm_g, final_g,
                     a_w_in, a_q_norm, a_kv_norm, a_w_q_up, a_w_kv_up, a_w_out,
                     b_w_in, b_ln_g, b_ln_b, b_w_s, b_b_s, b_w_out)
    return (y_prompt, y_sample)
```

```python
import numpy as np
from contextlib import ExitStack
import concourse.bass as bass
import concourse.mybir as mybir
from concourse.bass_utils import run_bass_kernel_spmd

F32 = mybir.dt.float32
BF16 = mybir.dt.bfloat16
AF = mybir.ActivationFunctionType
ALU = mybir.AluOpType

D = 1024
A_IN = 1728
NCORES = 8
QB = 2048
TT = 512
GT = 256
SCALE = 192.0 ** -0.5


class Sched:
    def __init__(self):
        self.ops = []
        self.deps = []
        self.last_w = {}
        self.readers = {}
        self.last_by_class = {}
        self.pending = {}
        self.incs = {}

    def _cls(self, i):
        eng, fn, dma = self.ops[i]
        return ('dma', dma) if dma is not None else ('eng', eng)

    def op(self, engine, method, kw, reads=(), writes=(), dma=None, inc=16):
        fn = (method, kw)
        ps = [r for r in reads if isinstance(r, tuple) and r[0] == 'ps']
        if ps:
            reads = [r for r in reads if r not in ps]
            writes = list(writes) + ps
        i = len(self.ops)
        d = set()
        for r in reads:
            d.update(self.last_w.get(r, {}).values())
        for w in writes:
            d.update(self.last_w.get(w, {}).values())
            d.update(self.readers.get(w, {}).values())
        if engine in self.pending:
            d.update(self.pending.pop(engine))
        self.ops.append((engine, fn, dma))
        self.incs[i] = inc
        c = self._cls(i)
        if engine == 'pe':
            d = {j for j in d if self._cls(j) != ('eng', 'pe')}
        self.deps.append(d)
        for r in reads:
            self.readers.setdefault(r, {})[c] = i
        for w in writes:
            self.last_w[w] = {c: i}
            self.readers[w] = {}
        self.last_by_class[c] = i
        return i

    def barrier(self):
        lasts = set(self.last_by_class.values())
        for e in ('pe', 'act', 'dve', 'pool', 'sp'):
            self.pending[e] = set(lasts) | self.pending.get(e, set())

    def finalize(self):
        n = len(self.ops)
        sig = [False] * n
        for d in self.deps:
            for j in d:
                sig[j] = True
        cnt = {}
        self.sigval = [0] * n
        self.semkey = [None] * n
        for i, (eng, fn, dma) in enumerate(self.ops):
            if dma is not None:
                k = ('dma', dma)
                cnt[k] = cnt.get(k, 0) + self.incs[i]
            elif sig[i]:
                k = ('eng', eng)
                cnt[k] = cnt.get(k, 0) + 1
            else:
                continue
            self.sigval[i] = cnt[k]
            self.semkey[i] = k
        self.final_counts = cnt
        return sorted(cnt.keys(), key=str)

    def emit(self, engine, eh, sems, final_wait_dma=False):
        waited = {}
        for i, (eng, fn, dma) in enumerate(self.ops):
            if eng != engine:
                continue
            need = {}
            for j in self.deps[i]:
                k = self.semkey[j]
                if need.get(k, 0) < self.sigval[j]:
                    need[k] = self.sigval[j]
            for k, v in need.items():
                if waited.get(k, 0) >= v:
                    continue
                eh.wait_ge(sems[k], v)
                waited[k] = v
            ins = getattr(eh, fn[0])(**fn[1])
            k = self.semkey[i]
            if k is not None:
                ins.then_inc(sems[k], self.incs[i] if dma is not None else 1)
        if final_wait_dma:
            for k, v in self.final_counts.items():
                if k[0] == 'dma':
                    eh.wait_ge(sems[k], v)


NAMES = {}


class Arena:
    def __init__(self, nc, base=16512, limit=229344):
        self.nc, self.top, self.n, self.limit, self.peak = nc, base, 0, limit, 0

    def alloc(self, name, shape, dt):
        nb = int(np.prod(shape[1:])) * (2 if dt == BF16 else 4)
        off = (self.top + 63) // 64 * 64
        assert off + nb <= self.limit, (name, off, nb)
        self.top = off + nb
        self.peak = max(self.peak, self.top)
        self.n += 1
        h = self.nc.alloc_sbuf_tensor_at(f"{name}_{self.n}", list(shape), dt, offset=off)
        NAMES.setdefault(name, []).append(h.name)
        return h

    def mark(self):
        return self.top

    def release(self, m):
        self.top = m


class B:
    pass


def build(cfg):
    nc = bass.Bass("TRN2", target_bir_lowering=False)
    S = Sched()
    A = Arena(nc)
    b = B()
    streams = cfg['streams']
    depth = cfg.get('depth', 4)

    def din(name, shape, dt=F32):
        return nc.dram_tensor(name, list(shape), dt, kind="ExternalInput").ap()

    def dscr(name, shape, dt):
        return nc.dram_tensor(name, list(shape), dt, kind="Internal").ap()

    for st in streams:
        n = st['n_own']
        st['x'] = din("x_" + st['name'], [n, D])
        st['cs'] = din("cs_" + st['name'], [128, 2, n])
        st['y'] = nc.dram_tensor("y_" + st['name'], [n, D], F32, kind="ExternalOutput").ap()
        st['scr'] = [dscr(f"xs{i}_" + st['name'], [n, D], F32) for i in range(2)]
    for st in streams:
        if st.get('exch'):
            n = st['n_own']
            st['xin'] = [dscr(f"xin{j}_" + st['name'], [128, 2 * n + n // 2], BF16) for j in range(2)]
            st['xout'] = [dscr(f"xout{j}_" + st['name'], [(st['skv'] // n) * 128, 2 * n + n // 2], BF16) for j in range(2)]
    a_w_in = din("a_w_in", [2, D, A_IN])
    a_w_q_up = din("a_w_q_up", [2, 384, 1536])
    a_w_kv_up = din("a_w_kv_up", [2, 256, 2048])
    a_w_out = din("a_w_out", [2, D, D])
    b_w_in = din("b_w_in", [2, D, 6144])
    b_w_out = din("b_w_out", [2, 2048, D])
    b_w_sT = din("b_w_sT", [2, 128, 1024])
    b_b_s = din("b_b_s", [2, 1024])
    vecs_d = din("vecs", [128, 128])
    final_g = din("final_g", [D])
    ident_d = din("ident", [128, 128])
    NL = 2
    wA = [dscr(f"wA{j}", [128, 8 * 512], BF16) for j in range(NL)]
    wQ = [dscr(f"wQ{j}", [128, 8 * 1408], BF16) for j in range(NL)]
    wBq = [dscr(f"wBq{j}", [128, 3 * 3072], BF16) for j in range(NL)]
    wBkv = [dscr(f"wBkv{j}", [128, 2 * 2048], BF16) for j in range(NL)]
    wC = [dscr(f"wC{j}", [128, 8 * 1024], BF16) for j in range(NL)]
    wGin = [dscr(f"wGin{j}", [128, 8 * 6144], BF16) for j in range(NL)]
    wGout = [dscr(f"wGout{j}", [128, 16 * 1024], BF16) for j in range(NL)]
    wsT = [dscr(f"wsT{j}", [128, 1024], BF16) for j in range(NL)]
    BTd = [dscr(f"BT{j}", [128, 2048], F32) for j in range(NL)]

    banks = [nc.alloc_psum_tensor(f"bank{i}", [128, 512], F32) for i in range(8)]
    b.bank_rr = 0

    def getbank(pool=(0, 1, 2, 3, 4, 5, 6, 7)):
        k = pool[b.bank_rr % len(pool)]
        b.bank_rr += 1
        return k

    def bankbf(k):
        return banks[k][:].bitcast(BF16)

    ident = A.alloc("ident", [128, 128], BF16)
    ones = A.alloc("ones", [128, 128], BF16)
    onesf = A.alloc("onesf", [128, 128], F32)
    epsb = A.alloc("epsb", [128, 1], F32)
    eps5 = A.alloc("eps5", [128, 1], F32)
    vecs = A.alloc("vecs", [128, 128], F32)
    nvecs = A.alloc("nvecs", [128, 128], F32)
    fgb = A.alloc("fgb", [128, D], F32)
    junk = A.alloc("junk", [128, D], BF16)
    VNG, VQN, VKVN, VLNG, VLNB = 0, 32, 38, 42, 74

    m0 = A.mark()
    idf = A.alloc("idf", [128, 128], F32)
    S.op('sp', 'dma_start', dict(out=idf[:], in_=ident_d), writes=['idf'], dma=('stf', 0))
    S.op('sp', 'dma_start', dict(out=vecs[:], in_=vecs_d), writes=['vecs'], dma=('stf', 1))
    S.op('sp', 'dma_start', dict(out=fgb[:], in_=final_g.partition_broadcast(128)), writes=['fgb'], dma=('stf', 2))
    S.op('dve', 'tensor_copy', dict(out=ident[:], in_=idf[:]), reads=['idf'], writes=['ident'])
    S.op('pool', 'memset', dict(ap=ones[:], constant=1.0), writes=['ones'])
    S.op('pool', 'memset', dict(ap=onesf[:], constant=1.0), writes=['onesf'])
    S.op('pool', 'memset', dict(ap=epsb[:], constant=1e-6), writes=['epsb'])
    S.op('pool', 'memset', dict(ap=eps5[:], constant=1e-5), writes=['eps5'])
    S.op('dve', 'tensor_scalar', dict(out=nvecs[:], in0=vecs[:], scalar1=-1.0, scalar2=None, op0=ALU.mult),
         reads=['vecs'], writes=['nvecs'])

    S.barrier()
    stf = [A.alloc("stf", [128, 2048], F32) for _ in range(3)]
    stb = [A.alloc("stb", [128, 3072], BF16) for _ in range(3)]
    b.pp = 0

    def piece(loads, convs, dst, ncols_out):
        i = b.pp
        b.pp += 1
        s = i % 3
        eng = 'dve' if i % 2 == 0 else 'act'
        for (c0, ncl, ap) in loads:
            S.op('sp', 'dma_start', dict(out=stf[s][:, c0:c0 + ncl], in_=ap),
                 writes=[('stf', s)], dma=('stf', s))
        for (ov, iv, sc) in convs:
            o_ap, i_ap = ov(stb[s]), iv(stf[s])
            if sc is None:
                if eng == 'dve':
                    fn = 'tensor_copy', dict(out=o_ap, in_=i_ap)
                else:
                    fn = 'activation', dict(out=o_ap, in_=i_ap, func=AF.Copy)
            else:
                if eng == 'dve':
                    fn = 'tensor_scalar', dict(out=o_ap, in0=i_ap, scalar1=sc, scalar2=None, op0=ALU.mult)
                else:
                    fn = 'activation', dict(out=o_ap, in_=i_ap, func=AF.Copy, scale=sc)
            S.op(eng, fn[0], fn[1], reads=[('stf', s), 'vecs', 'nvecs'], writes=[('stb', s)])
        S.op('pool', 'dma_start', dict(out=dst, in_=stb[s][:, 0:ncols_out]), reads=[('stb', s)], dma=('stbo', s))

    def sl(c0, c1):
        return lambda t: t[:, c0:c1]

    for j in range(NL):
        if cfg.get('dbg') == 'nopiece':
            break
        lyr = 2 * j
        for c in range(8):
            g = vecs[:, VNG + lyr * 8 + c:VNG + lyr * 8 + c + 1]
            ngm = nvecs[:, VNG + lyr * 8 + c:VNG + lyr * 8 + c + 1]
            rows = slice(c * 128, (c + 1) * 128)
            piece([(0, 320, a_w_in[j, rows, 384:704])],
                  [(sl(0, 256), sl(0, 256), g), (sl(256, 320), sl(256, 320), g), (sl(320, 384), sl(256, 320), g),
                   (sl(384, 416), sl(288, 320), ngm), (sl(416, 448), sl(256, 288), g),
                   (sl(448, 480), sl(288, 320), ngm), (sl(480, 512), sl(256, 288), g)],
                  wA[j][:, c * 512:(c + 1) * 512], 512)
            piece([(0, 384, a_w_in[j, rows, 0:384]), (384, 1024, a_w_in[j, rows, 704:1728])],
                  [(sl(0, 1408), sl(0, 1408), g)], wQ[j][:, c * 1408:(c + 1) * 1408], 1408)
        for c in range(3):
            g = vecs[:, VQN + j * 3 + c:VQN + j * 3 + c + 1]
            ngm = nvecs[:, VQN + j * 3 + c:VQN + j * 3 + c + 1]
            iv = lambda a, bb: (lambda t: t[:, 0:1536].rearrange("p (h x) -> p h x", h=8)[:, :, a:bb])
            ov = lambda a, bb: (lambda t: t[:, 0:3072].rearrange("p (h x) -> p h x", h=8)[:, :, a:bb])
            piece([(0, 1536, a_w_q_up[j, c * 128:(c + 1) * 128, :])],
                  [(ov(0, 128), iv(0, 128), g), (ov(128, 192), iv(128, 192), g), (ov(192, 256), iv(128, 192), g),
                   (ov(256, 288), iv(160, 192), ngm), (ov(288, 320), iv(128, 160), g),
                   (ov(320, 352), iv(160, 192), ngm), (ov(352, 384), iv(128, 160), g)],
                  wBq[j][:, c * 3072:(c + 1) * 3072], 3072)
        for c in range(2):
            g = vecs[:, VKVN + j * 2 + c:VKVN + j * 2 + c + 1]
            piece([(0, 2048, a_w_kv_up[j, c * 128:(c + 1) * 128, :])], [(sl(0, 2048), sl(0, 2048), g)],
                  wBkv[j][:, c * 2048:(c + 1) * 2048], 2048)
        for c in range(0, 8, 2):
            piece([(0, 1024, a_w_out[j, c * 128:(c + 1) * 128, :]), (1024, 1024, a_w_out[j, (c + 1) * 128:(c + 2) * 128, :])],
                  [(sl(0, 2048), sl(0, 2048), None)], wC[j][:, c * 1024:(c + 2) * 1024], 2048)
        lyr = 2 * j + 1
        for c in range(8):
            g = vecs[:, VNG + lyr * 8 + c:VNG + lyr * 8 + c + 1]
            for n3 in range(3):
                piece([(0, 2048, b_w_in[j, c * 128:(c + 1) * 128, n3 * 2048:(n3 + 1) * 2048])],
                      [(sl(0, 2048), sl(0, 2048), g)], wGin[j][:, c * 6144 + n3 * 2048:c * 6144 + (n3 + 1) * 2048], 2048)
        for c in range(0, 16, 2):
            piece([(0, 1024, b_w_out[j, c * 128:(c + 1) * 128, :]), (1024, 1024, b_w_out[j, (c + 1) * 128:(c + 2) * 128, :])],
                  [(sl(0, 2048), sl(0, 2048), 0.5)], wGout[j][:, c * 1024:(c + 2) * 1024], 2048)
        piece([(0, 1024, b_w_sT[j])], [(sl(0, 1024), sl(0, 1024), None)], wsT[j][:, :], 1024)
        if cfg.get('dbg') == 'nobt':
            continue
        wsf = A.alloc("wsf", [128, 1024], F32)
        bsb = A.alloc("bsb", [128, 1024], F32)
        btt = A.alloc("btt", [128, 2048], F32)
        S.op('sp', 'dma_start', dict(out=wsf[:], in_=b_w_sT[j]), writes=['wsf'], dma=('xr', 0))
        S.op('sp', 'dma_start', dict(out=bsb[:], in_=b_b_s[j].partition_broadcast(128)), writes=['bsb'], dma=('xr', 1))
        whi = A.alloc("whi", [128, 1024], BF16)
        wlo = A.alloc("wlo", [128, 1024], BF16)
        S.op('dve', 'tensor_copy', dict(out=whi[:], in_=wsf[:]), reads=['wsf'], writes=['whi'])
        S.op('dve', 'tensor_tensor', dict(out=wlo[:], in0=wsf[:], in1=whi[:], op=ALU.subtract), reads=['wsf', 'whi'], writes=['wlo'])
        for hlf in range(2):
            k = getbank()
            S.op('pe', 'matmul', dict(out=banks[k][:], lhsT=ones[:], rhs=whi[:, hlf * 512:(hlf + 1) * 512], start=True, stop=False),
                 reads=['ones', 'whi'], writes=[('ps', k)])
            S.op('pe', 'matmul', dict(out=banks[k][:], lhsT=ones[:], rhs=wlo[:, hlf * 512:(hlf + 1) * 512], start=False, stop=True),
                 reads=['ones', 'wlo'], writes=[('ps', k)])
            for gg in range(4):
                g8 = hlf * 4 + gg
                for cc in range(2):
                    c16 = g8 * 2 + cc
                    S.op('dve', 'scalar_tensor_tensor', dict(
                        out=btt[:, c16 * 128:(c16 + 1) * 128], in0=banks[k][:, gg * 128:(gg + 1) * 128],
                        scalar=vecs[:, VLNB + j * 16 + c16:VLNB + j * 16 + c16 + 1], in1=bsb[:, g8 * 128:(g8 + 1) * 128],
                        op0=ALU.mult, op1=ALU.add), reads=[('ps', k), 'bsb', 'vecs'], writes=['btt'])
        S.op('pool', 'dma_start', dict(out=BTd[j][:, :], in_=btt[:]), reads=['btt'], dma=('stbo', 0))
        S.barrier()
        A.release(A.mark() - 0)
    S.barrier()
    A.release(m0)

    b.xc = 0

    def front(fb, src, tok0, ntok, hT, hTres):
        for s in range(ntok // 128):
            i = b.xc
            b.xc += 1
            xs, hs = i % fb.nx, i % 2
            xt, hb, ss = fb.xt[xs], fb.hb[hs], fb.ss[xs]
            r0 = tok0 + s * 128
            S.op('sp', 'dma_start', dict(out=xt[:], in_=src[r0:r0 + 128, :]), writes=[('xt', xs)], dma=('xt', xs))
            S.op('act', 'activation', dict(out=junk[:], in_=xt[:], func=AF.Square, scale=1.0 / 32, accum_out=ss[:, 0:1]),
                 reads=[('xt', xs)], writes=['junk', ('ss', xs)])
            S.op('act', 'activation', dict(out=ss[:, 1:2], in_=ss[:, 0:1], func=AF.Ln, bias=epsb[:]),
                 reads=[('ss', xs), 'epsb'], writes=[('ss', xs)])
            S.op('act', 'activation', dict(out=ss[:, 2:3], in_=ss[:, 1:2], func=AF.Exp, scale=-0.5),
                 reads=[('ss', xs)], writes=[('ss', xs)])
            S.op('dve', 'tensor_scalar', dict(out=hb[:], in0=xt[:], scalar1=ss[:, 2:3], scalar2=None, op0=ALU.mult),
                 reads=[('xt', xs), ('ss', xs)], writes=[('hb', hs)])
            k = getbank()
            tv = bankbf(k).rearrange("p (c t) -> p c t", c=8)
            for c in range(8):
                S.op('pe', 'transpose', dict(out=tv[:, c, :], in_=hb[:, c * 128:(c + 1) * 128], identity=ident[:]),
                     reads=[('hb', hs), 'ident'], writes=[('ps', k)])
            S.op('dve', 'tensor_copy', dict(out=hT[:, :, s * 128:(s + 1) * 128], in_=tv),
                 reads=[('ps', k)], writes=[hTres + (s,)])

    def alloc_front(ntok, nx=3):
        fb = B()
        fb.nx = nx
        fb.xt = [A.alloc("xt", [128, D], F32) for _ in range(nx)]
        fb.hb = [A.alloc("hb", [128, D], BF16) for _ in range(2)]
        fb.ss = [A.alloc("ss", [128, 4], F32) for _ in range(3)]
        fb.hT = [A.alloc("hT", [128, 8, ntok], BF16) for _ in range(2)]
        fb.n = 0
        return fb

    def rstd_bc(k, nfeat, out_ap, res_w):
        S.op('act', 'activation', dict(out=out_ap, in_=banks[k][:], func=AF.Ln, scale=1.0 / nfeat, bias=epsb[:]),
             reads=[('ps', k), 'epsb'], writes=[res_w])
        S.op('act', 'activation', dict(out=out_ap, in_=out_ap, func=AF.Exp, scale=-0.5), reads=[res_w], writes=[res_w])

    def load_w(dst, src, res, key):
        S.op('sp', 'dma_start', dict(out=dst, in_=src), writes=[res], dma=key)

    def mla_layer(st, j, src, dst):
        n_own, skv = st['n_own'], st['skv']
        exch = bool(st.get('exch'))
        QBl = min(QB, n_own)
        mL = A.mark()
        kvT = A.alloc("kvT", [128, 2, skv], BF16)
        krT = A.alloc("krT", [128, skv // 2], BF16)
        wreg = A.alloc("wreg", [128, 13312], BF16)
        mP = A.mark()
        kvT_all, krT_all = kvT, krT
        if exch:
            kvT = A.alloc("kvTo", [128, 2, n_own], BF16)
            krT = A.alloc("krTo", [128, n_own // 2], BF16)
        RK, RR = ('kvTo', 'krTo') if exch else ('kvT', 'krT')
        fb = alloc_front(TT)
        wa = wreg[:, 0:4096].rearrange("p (c n) -> p c n", c=8)
        load_w(wreg[:, 0:4096], wA[j][:, :], 'wreg', 'wreg')
        cst = [A.alloc("cst", [128, 2, TT], F32) for _ in range(2)]
        sqb = [A.alloc("sqb", [128, TT], BF16) for _ in range(2)]
        raw = [A.alloc("raw", [128, TT], BF16) for _ in range(2)]
        rbc = A.alloc("rbc", [128, TT], F32)
        t1 = A.alloc("t1", [128, TT], F32)
        t2 = A.alloc("t2", [128, TT], F32)
        for t in range(n_own // TT):
            tok0 = t * TT
            hT = fb.hT[t % 2]
            hres = ('hT', t % 2)
            front(fb, src, tok0, TT, hT, hres)
            hreads = [hres + (s,) for s in range(4)]
            if cfg.get('dbg') == 'A1':
                continue
            cs = cst[t % 2]
            if cfg.get('dbg') != 'A2b':
                S.op('sp', 'dma_start', dict(out=cs[:], in_=st['cs'][:, :, tok0:tok0 + TT]),
                     writes=[('cst', t % 2)], dma=('cq', t % 2))
            kb = []
            for m in range(4):
                k = getbank()
                kb.append(k)
                for c in range(8):
                    S.op('pe', 'matmul', dict(out=banks[k][:], lhsT=wa[:, c, m * 128:(m + 1) * 128], rhs=hT[:, c, :],
                                                                       start=(c == 0), stop=(c == 7)),
                         reads=hreads + ['wreg'], writes=[('ps', k)])
            for m in range(2):
                S.op('act', 'activation', dict(out=sqb[m][:], in_=banks[kb[m]][:], func=AF.Square),
                     reads=[('ps', kb[m])], writes=[('sqb', m)])
                S.op('dve', 'tensor_copy', dict(out=raw[m][:], in_=banks[kb[m]][:]), reads=[('ps', kb[m])], writes=[('raw', m)])
            kd = getbank()
            for m in range(2):
                if cfg.get('dbg') == 'A2c':
                    continue
                S.op('pe', 'matmul', dict(out=banks[kd][:], lhsT=ones[:], rhs=sqb[m][:], start=(m == 0), stop=(m == 1)),
                     reads=[('sqb', m), 'ones'], writes=[('ps', kd)])
            if cfg.get('dbg') in ('A2', 'A2b', 'A2c'):
                continue
            rstd_bc(kd, 256, rbc[:], 'rbc')
            for m in range(2):
                S.op('dve', 'tensor_tensor', dict(out=kvT[:, m, tok0:tok0 + TT], in0=raw[m][:], in1=rbc[:], op=ALU.mult),
                     reads=[('raw', m), 'rbc'], writes=[(RK, t)])
            S.op('dve', 'tensor_tensor', dict(out=t1[:], in0=banks[kb[2]][:], in1=cs[:, 0, :], op=ALU.mult),
                 reads=[('ps', kb[2]), ('cst', t % 2)], writes=['t1'])
            S.op('dve', 'tensor_tensor', dict(out=t2[:], in0=banks[kb[3]][:], in1=cs[:, 1, :], op=ALU.mult),
                 reads=[('ps', kb[3]), ('cst', t % 2)], writes=['t2'])
            for hf in range(2):
                if cfg.get('dbg') == 'A3':
                    continue
                S.op('pool', 'tensor_tensor', dict(
                    out=krT[64 * hf:64 * hf + 64, 2 * t * 128:(2 * t + 2) * 128].rearrange("p (i x) -> p i x", i=2),
                    in0=t1[64 * hf:64 * hf + 64, :].rearrange("p (i two x) -> p i two x", i=2, two=2)[:, :, hf, :],
                    in1=t2[64 * hf:64 * hf + 64, :].rearrange("p (i two x) -> p i two x", i=2, two=2)[:, :, hf, :],
                    op=ALU.add), reads=['t1', 't2'], writes=[(RR, t, hf)])
        if exch:
            nt = n_own // TT
            allw = [(RK, t) for t in range(nt)] + [(RR, t, hf) for t in range(nt) for hf in range(2)]
            xin, xout = st['xin'][j], st['xout'][j]
            W = 2 * n_own + n_own // 2
            S.op('pool', 'dma_start', dict(out=xin[:, 0:2 * n_own].rearrange("p (c n) -> p c n", c=2), in_=kvT[:]), reads=allw, dma='xi')
            S.op('pool', 'dma_start', dict(out=xin[:, 2 * n_own:W], in_=krT[:]), reads=allw, writes=['xin'], dma='xi')
            S.barrier()
            S.op('pool', 'collective_compute', dict(kind="AllGather", op=ALU.bypass, replica_groups=[list(range(NCORES))],
                                                   ins=[xin[:, :]], outs=[xout[:, :]]), reads=['xin'], writes=['xout'], dma=('cc', j), inc=1)
            S.barrier()
            kvT, krT = kvT_all, krT_all
            for r in range(skv // n_own):
                S.op('sp', 'dma_start', dict(out=kvT[:, :, r * n_own:(r + 1) * n_own],
                                             in_=xout[r * 128:(r + 1) * 128, 0:2 * n_own].rearrange("p (c n) -> p c n", c=2)),
                     reads=['xout'], writes=[('kvT', r * nt + t) for t in range(nt)], dma='xo')
                S.op('sp', 'dma_start', dict(out=krT[:, r * (n_own // 2):(r + 1) * (n_own // 2)], in_=xout[r * 128:(r + 1) * 128, 2 * n_own:W]),
                     reads=['xout'], writes=[('krT', r * nt + t, hf) for t in range(nt) for hf in range(2)], dma='xo')
        S.barrier()
        A.release(mP)
        if cfg.get('dbg') in ('A', 'A1', 'A2', 'A3', 'A2b', 'A2c'):
            return
        for qb in range(n_own // QBl):
            q0 = qb * QBl
            mQ = A.mark()
            qlT = A.alloc("qlT", [128, 3, QBl], BF16)
            GT_ = A.alloc("GT", [128, 8, QBl], BF16)
            mP = A.mark()
            fb = alloc_front(TT)
            wq = wreg[:, 0:11264].rearrange("p (c n) -> p c n", c=8)
            load_w(wreg[:, 0:11264], wQ[j][:, :], 'wreg', 'wreg')
            sqb = [A.alloc("sqb", [128, TT], BF16) for _ in range(3)]
            raw = [A.alloc("raw", [128, TT], BF16) for _ in range(3)]
            rbc = A.alloc("rbc", [128, TT], F32)
            for t in range(QBl // TT):
                tok0 = q0 + t * TT
                l0 = t * TT
                hT = fb.hT[t % 2]
                hres = ('hT', t % 2)
                front(fb, src, tok0, TT, hT, hres)
                hreads = [hres + (s,) for s in range(4)]
                kb = []
                for m in range(3):
                    k = getbank()
                    kb.append(k)
                    for c in range(8):
                        S.op('pe', 'matmul', dict(out=banks[k][:], lhsT=wq[:, c, m * 128:(m + 1) * 128], rhs=hT[:, c, :],
                                                                           start=(c == 0), stop=(c == 7)),
                             reads=hreads + ['wreg'], writes=[('ps', k)])
                    S.op('act', 'activation', dict(out=sqb[m][:], in_=banks[k][:], func=AF.Square),
                         reads=[('ps', k)], writes=[('sqb', m)])
                    S.op('dve', 'tensor_copy', dict(out=raw[m][:], in_=banks[k][:]), reads=[('ps', k)], writes=[('raw', m)])
                kd = getbank()
                for m in range(3):
                    S.op('pe', 'matmul', dict(out=banks[kd][:], lhsT=ones[:], rhs=sqb[m][:], start=(m == 0), stop=(m == 2)),
                         reads=[('sqb', m), 'ones'], writes=[('ps', kd)])
                rstd_bc(kd, 384, rbc[:], 'rbc')
                for m in range(3):
                    S.op('dve', 'tensor_tensor', dict(out=qlT[:, m, l0:l0 + TT], in0=raw[m][:], in1=rbc[:], op=ALU.mult),
                         reads=[('raw', m), 'rbc'], writes=[('qlT', t)])
                for m in range(8):
                    k = getbank()
                    for c in range(8):
                        S.op('pe', 'matmul', dict(out=banks[k][:], lhsT=wq[:, c, 384 + m * 128:384 + (m + 1) * 128],
                                                                           rhs=hT[:, c, :], start=(c == 0), stop=(c == 7)),
                             reads=hreads + ['wreg'], writes=[('ps', k)])
                    S.op('act', 'activation', dict(out=GT_[:, m, l0:l0 + TT], in_=banks[k][:], func=AF.Silu),
                         reads=[('ps', k)], writes=[('GT', m, t)])
            S.barrier()
            A.release(mP)
            if cfg.get('dbg') == 'Q':
                return
            mP = A.mark()
            wbq = wreg[:, 0:9216].rearrange("p (c h x) -> p c h x", c=3, h=8)
            wbkv = wreg[:, 9216:13312].rearrange("p (c n) -> p c n", c=2)
            load_w(wreg[:, 0:9216], wBq[j][:, :], 'wreg', 'wreg')
            load_w(wreg[:, 9216:13312], wBkv[j][:, :], 'wreg2', 'wreg2')
            Kh = A.alloc("Kh", [128, skv], BF16)
            Vh = A.alloc("Vh", [128, skv // 128, 128], BF16)
            qn = A.alloc("qn", [128, QBl], BF16)
            qrA = A.alloc("qrA", [128, QBl], BF16)
            qrB = A.alloc("qrB", [128, QBl], BF16)
            NPT = 12
            PT = [A.alloc("PT", [128, TT], BF16) for _ in range(NPT)]
            cq = [A.alloc("cq", [128, 2, TT], F32) for _ in range(2)]
            t1 = A.alloc("t1", [128, TT], F32)
            t2 = A.alloc("t2", [128, TT], F32)
            acc1 = A.alloc("acc1", [128, TT], F32)
            acc2 = A.alloc("acc2", [128, TT], F32)
            ahi = A.alloc("ahi", [128, TT], BF16)
            alo = A.alloc("alo", [128, TT], BF16)
            rden = A.alloc("rden", [128, TT], F32)
            nqt = QBl // TT
            S.op('pool', 'memset', dict(ap=qrA[:], constant=0.0), writes=[('qrA', qt) for qt in range(nqt)])
            S.op('pool', 'memset', dict(ap=qrB[:], constant=0.0), writes=[('qrB', qt) for qt in range(nqt)])
            b.cq = 0
            P4 = (0, 1, 2, 3, 5)
            nkt = skv // 128
            b.pt = 0
            b.ev = 0
            for h in range(8):
                for kt in range(skv // TT):
                    k = getbank(P4)
                    for c in range(2):
                        S.op('pe', 'matmul', dict(out=banks[k][:], lhsT=wbkv[:, c, h * 256:h * 256 + 128],
                                                                           rhs=kvT[:, c, kt * TT:(kt + 1) * TT], start=(c == 0), stop=(c == 1)),
                             reads=['wreg2', ('kvT', kt)], writes=[('ps', k)])
                    b.ev += 1
                    if b.ev % 2:
                        S.op('dve', 'tensor_copy', dict(out=Kh[:, kt * TT:(kt + 1) * TT], in_=banks[k][:]),
                             reads=[('ps', k)], writes=[('Kh', kt)])
                    else:
                        S.op('act', 'activation', dict(out=Kh[:, kt * TT:(kt + 1) * TT], in_=banks[k][:], func=AF.Copy),
                             reads=[('ps', k)], writes=[('Kh', kt)])
                for g4 in range(nkt // 4):
                    k = getbank(P4)
                    for i4 in range(4):
                        jt = g4 * 4 + i4
                        for c in range(2):
                            S.op('pe', 'matmul', dict(
                                out=banks[k][:, i4 * 128:(i4 + 1) * 128], lhsT=kvT[:, c, jt * 128:(jt + 1) * 128],
                                rhs=wbkv[:, c, h * 256 + 128:h * 256 + 256], start=(c == 0), stop=(c == 1)),
                                reads=['wreg2', ('kvT', jt // 4)], writes=[('ps', k)])
                    b.ev += 1
                    if b.ev % 2:
                        S.op('dve', 'tensor_copy', dict(out=Vh[:, g4 * 4:(g4 + 1) * 4, :],
                                                                        in_=banks[k][:].rearrange("p (i x) -> p i x", i=4)),
                             reads=[('ps', k)], writes=[('Vh', g4)])
                    else:
                        S.op('act', 'activation', dict(out=Vh[:, g4 * 4:(g4 + 1) * 4, :],
                                                                       in_=banks[k][:].rearrange("p (i x) -> p i x", i=4), func=AF.Copy),
                             reads=[('ps', k)], writes=[('Vh', g4)])
                for qt in range(QBl // TT):
                    l0 = qt * TT
                    kk = []
                    for part in range(3):
                        k = getbank(P4)
                        kk.append(k)
                        for c in range(3):
                            S.op('pe', 'matmul', dict(
                                out=banks[k][:], lhsT=wbq[:, c, h, part * 128:(part + 1) * 128], rhs=qlT[:, c, l0:l0 + TT],
                                start=(c == 0), stop=(c == 2)), reads=['wreg', ('qlT', qt)], writes=[('ps', k)])
                    S.op('act', 'activation', dict(out=qn[:, l0:l0 + TT], in_=banks[kk[0]][:], func=AF.Copy),
                         reads=[('ps', kk[0])], writes=[('qn', qt)])
                    ci = b.cq % 2
                    b.cq += 1
                    S.op('sp', 'dma_start', dict(out=cq[ci][:], in_=st['cs'][:, :, q0 + l0:q0 + l0 + TT]), writes=[('cq', ci)], dma=('cq', ci))
                    S.op('dve', 'tensor_tensor', dict(out=t1[:], in0=banks[kk[1]][:], in1=cq[ci][:, 0, :], op=ALU.mult),
                         reads=[('ps', kk[1]), ('cq', ci)], writes=['t1'])
                    S.op('dve', 'tensor_tensor', dict(out=t2[:], in0=banks[kk[2]][:], in1=cq[ci][:, 1, :], op=ALU.mult),
                         reads=[('ps', kk[2]), ('cq', ci)], writes=['t2'])
                    S.op('pool', 'tensor_tensor', dict(out=qrA[0:64, l0:l0 + TT], in0=t1[0:64, :], in1=t2[0:64, :], op=ALU.add),
                         reads=['t1', 't2'], writes=[('qrA', qt)])
                    S.op('pool', 'tensor_tensor', dict(out=qrB[64:128, l0:l0 + TT], in0=t1[64:128, :], in1=t2[64:128, :], op=ALU.add),
                         reads=['t1', 't2'], writes=[('qrB', qt)])
                for qt in range(QBl // TT):
                    l0 = qt * TT
                    ko = 4 + (2 * (h * 4 + qt)) % 4
                    kd = 7
                    pend = []

                    first = {'dve': True, 'pool': True}

                    def pv(item, last):
                        jt, sl_ = item
                        S.op('pe', 'matmul', dict(out=banks[ko][:], lhsT=Vh[:, jt, :], rhs=PT[sl_][:], start=(jt == 0), stop=last),
                             reads=[('Vh', jt // 4), ('PT', sl_)], writes=[('ps', ko)])
                        eng, acc, res = ('pool', acc2, 'acc2') if jt % 2 == 1 else ('dve', acc1, 'acc1')
                        if first[eng]:
                            first[eng] = False
                            S.op(eng, 'tensor_copy', dict(out=acc[:], in_=PT[sl_][:]), reads=[('PT', sl_)], writes=[res])
                        else:
                            S.op(eng, 'tensor_tensor', dict(out=acc[:], in0=acc[:], in1=PT[sl_][:], op=ALU.add), reads=[('PT', sl_), res], writes=[res])

                    for jt in range(nkt):
                        k = getbank(P4)
                        hf = jt % 2
                        S.op('pe', 'matmul', dict(out=banks[k][:], lhsT=Kh[:, jt * 128:(jt + 1) * 128], rhs=qn[:, l0:l0 + TT],
                                                  start=True, stop=False),
                             reads=[('Kh', jt // 4), ('qn', qt)], writes=[('ps', k)])
                        qz, qres = (qrA, 'qrA') if hf == 0 else (qrB, 'qrB')
                        S.op('pe', 'matmul', dict(out=banks[k][:], lhsT=krT[:, (jt // 2) * 128:(jt // 2 + 1) * 128],
                                                  rhs=qz[:, l0:l0 + TT], start=False, stop=True),
                             reads=[('krT', jt // 4, 0), ('krT', jt // 4, 1), (qres, qt)], writes=[('ps', k)])
                        sl_ = b.pt % NPT
                        b.pt += 1
                        S.op('act', 'activation', dict(out=PT[sl_][:], in_=banks[k][:], func=AF.Exp, scale=SCALE),
                             reads=[('ps', k)], writes=[('PT', sl_)])
                        pend.append((jt, sl_))
                        if len(pend) > 3:
                            pv(pend.pop(0), False)
                    while pend:
                        it = pend.pop(0)
                        pv(it, len(pend) == 0)
                    S.op('dve', 'tensor_tensor', dict(out=acc1[:], in0=acc1[:], in1=acc2[:], op=ALU.add), reads=['acc1', 'acc2'], writes=['acc1'])
                    S.op('dve', 'tensor_copy', dict(out=ahi[:], in_=acc1[:]), reads=['acc1'], writes=['ahi'])
                    S.op('dve', 'tensor_tensor', dict(out=alo[:], in0=acc1[:], in1=ahi[:], op=ALU.subtract), reads=['acc1', 'ahi'], writes=['alo'])
                    S.op('pe', 'matmul', dict(out=banks[kd][:], lhsT=ones[:], rhs=ahi[:], start=True, stop=False), reads=['ones', 'ahi'], writes=[('ps', kd)])
                    S.op('pe', 'matmul', dict(out=banks[kd][:], lhsT=ones[:], rhs=alo[:], start=False, stop=True), reads=['ones', 'alo'], writes=[('ps', kd)])
                    S.op('dve', 'reciprocal', dict(out=rden[:], in_=banks[kd][:]), reads=[('ps', kd)], writes=['rden'])
                    S.op('dve', 'tensor_tensor', dict(out=rden[:], in0=banks[ko][:], in1=rden[:], op=ALU.mult),
                         reads=[('ps', ko), 'rden'], writes=['rden'])
                    S.op('pool', 'tensor_tensor', dict(out=GT_[:, h, l0:l0 + TT], in0=GT_[:, h, l0:l0 + TT], in1=rden[:], op=ALU.mult),
                         reads=['rden', ('GT', h, qt)], writes=[('GT', h, qt)])
            S.barrier()
            A.release(mP)
            if cfg.get('dbg') == 'B':
                return
            mP = A.mark()
            wc = wreg[:, 0:8192].rearrange("p (c n) -> p c n", c=8)
            load_w(wreg[:, 0:8192], wC[j][:, :], 'wreg', 'wreg')
            xr = [A.alloc("xr", [128, D], F32) for _ in range(3)]
            yo = [A.alloc("yo", [128, D], F32) for _ in range(2)]
            for s in range(QBl // 128):
                r0 = q0 + s * 128
                xs, ys = s % 3, s % 2
                S.op('sp', 'dma_start', dict(out=xr[xs][:], in_=src[r0:r0 + 128, :]), writes=[('xr', xs)], dma=('xr', xs))
                for nh in range(2):
                    k = getbank()
                    for c in range(8):
                        S.op('pe', 'matmul', dict(out=banks[k][:], lhsT=GT_[:, c, s * 128:(s + 1) * 128],
                                                                           rhs=wc[:, c, nh * 512:(nh + 1) * 512], start=(c == 0), stop=(c == 7)),
                             reads=[('GT', c, s // 4), 'wreg'], writes=[('ps', k)])
                    S.op('dve', 'tensor_tensor', dict(out=yo[ys][:, nh * 512:(nh + 1) * 512], in0=banks[k][:],
                                                                                  in1=xr[xs][:, nh * 512:(nh + 1) * 512], op=ALU.add),
                         reads=[('ps', k), ('xr', xs)], writes=[('yo', ys, nh)])
                S.op('pool', 'dma_start', dict(out=dst[r0:r0 + 128, :], in_=yo[ys][:]),
                     reads=[('yo', ys, 0), ('yo', ys, 1)], dma=('yo', ys))
            S.barrier()
            A.release(mP)
            A.release(mQ)
        A.release(mL)

    def gmlp_layer(j, srcs_dsts, final):
        mL = A.mark()
        win = A.alloc("win", [128, 8, 6144], BF16)
        wout = A.alloc("wout", [128, 16, 1024], BF16)
        wst = A.alloc("wst", [128, 8, 128], BF16)
        BT = A.alloc("BT", [128, 16, 128], F32)
        for c in range(8):
            load_w(win[:, c, :], wGin[j][:, c * 6144:(c + 1) * 6144], ('win', c), 'wreg')
        load_w(wout[:], wGout[j][:, :].rearrange("p (c n) -> p c n", c=16), 'wout', 'wreg')
        load_w(wst[:], wsT[j][:, :].rearrange("p (c n) -> p c n", c=8), 'wst', 'wreg')
        load_w(BT[:], BTd[j][:, :].rearrange("p (c n) -> p c n", c=16), 'BT', 'wreg')
        S.barrier()
        wreads = [('win', c) for c in range(8)]
        fb = alloc_front(GT, nx=2)
        NS = GT // 128
        gv = A.alloc("gv", [128, 2048], F32)
        vn = A.alloc("vn", [128, NS, 2048], BF16)
        stt = [A.alloc("stt", [128, 8], F32) for _ in range(2)]
        gu = [A.alloc("gu", [128, GT], BF16) for _ in range(2)]
        sg = [A.alloc("sg", [128, GT], BF16) for _ in range(2)]
        up = [A.alloc("up", [128, GT], BF16) for _ in range(2)]
        tz = [A.alloc("tz", [128, GT], F32) for _ in range(2)]
        zT = A.alloc("zT", [128, 16, GT], BF16)
        xr = [A.alloc("xr", [128, D], F32) for _ in range(2)]
        yo = xr
        fss = [A.alloc("fss", [128, 4], F32) for _ in range(3)]
        lng = vecs[:, VLNG + j * 16:VLNG + (j + 1) * 16]
        b.gi = 0
        b.gs = 0
        for (n_tok, src, dst) in srcs_dsts:
            for t in range(n_tok // GT):
                tok0 = t * GT
                ti = b.gi
                b.gi += 1
                hT = fb.hT[ti % 2]
                hres = ('hT', ti % 2)
                front(fb, src, tok0, GT, hT, hres)
                hreads = [hres + (s,) for s in range(NS)]
                for s in range(NS):
                    if cfg.get('dbg') == 'G0':
                        continue
                    si = b.gs
                    b.gs += 1
                    stt_ = stt[si % 2]
                    kv4 = []
                    for n4 in range(4):
                        k = getbank()
                        kv4.append(k)
                        for c in range(8):
                            S.op('pe', 'matmul', dict(
                                out=banks[k][:], lhsT=hT[:, c, s * 128:(s + 1) * 128], rhs=win[:, c, 2048 + n4 * 512:2048 + (n4 + 1) * 512],
                                start=(c == 0), stop=(c == 7)), reads=[hres + (s,)] + wreads, writes=[('ps', k)])
                        S.op('act', 'activation', dict(out=gv[:, n4 * 512:(n4 + 1) * 512], in_=banks[k][:], func=AF.Gelu,
                                                                                 accum_out=stt_[:, n4:n4 + 1]),
                             reads=[('ps', k)], writes=[('gv', n4), ('stt', si % 2, n4)])
                    if cfg.get('dbg') == 'G1a':
                        continue
                    S.op('act', 'activation', dict(out=vn[:, s, :], in_=gv[:], func=AF.Square, accum_out=stt_[:, 4:5]),
                         reads=[('gv', n4) for n4 in range(4)], writes=[('vn', s), ('stt', si % 2, 4)])
                    stall = [('stt', si % 2, q) for q in range(8)]
                    if cfg.get('dbg') == 'G1b':
                        continue
                    S.op('dve', 'tensor_reduce', dict(out=stt_[:, 5:6], in_=stt_[:, 0:4], axis=mybir.AxisListType.X, op=ALU.add),
                         reads=stall[0:4], writes=[stall[5]])
                    S.op('dve', 'tensor_scalar', dict(out=stt_[:, 5:6], in0=stt_[:, 5:6], scalar1=1.0 / 2048, scalar2=None, op0=ALU.mult),
                         reads=[stall[5]], writes=[stall[5]])
                    S.op('dve', 'tensor_tensor', dict(out=stt_[:, 6:7], in0=stt_[:, 5:6], in1=stt_[:, 5:6], op=ALU.mult),
                         reads=[stall[5]], writes=[stall[6]])
                    S.op('dve', 'scalar_tensor_tensor', dict(out=stt_[:, 6:7], in0=stt_[:, 4:5], scalar=1.0 / 2048, in1=stt_[:, 6:7],
                                                                          op0=ALU.mult, op1=ALU.subtract),
                         reads=[stall[4], stall[6]], writes=[stall[6]])
                    if cfg.get('dbg') == 'G1c':
                        continue
                    S.op('act', 'activation', dict(out=stt_[:, 7:8], in_=stt_[:, 6:7], func=AF.Ln, bias=eps5[:, 0:1], scale=1.0),
                         reads=[stall[6], 'eps5'], writes=[stall[7]])
                    S.op('act', 'activation', dict(out=stt_[:, 7:8], in_=stt_[:, 7:8], func=AF.Exp, scale=-0.5),
                         reads=[stall[7]], writes=[stall[7]])
                    if cfg.get('dbg') == 'G1d':
                        continue
                    S.op('dve', 'tensor_tensor', dict(out=stt_[:, 6:7], in0=stt_[:, 5:6], in1=stt_[:, 7:8], op=ALU.mult),
                         reads=[stall[5], stall[7]], writes=[stall[6]])
                    S.op('dve', 'tensor_scalar', dict(out=stt_[:, 6:7], in0=stt_[:, 6:7], scalar1=-1.0, scalar2=None, op0=ALU.mult),
                         reads=[stall[6]], writes=[stall[6]])
                    S.op('act', 'activation', dict(out=vn[:, s, :], in_=gv[:], func=AF.Identity, scale=stt_[:, 7:8], bias=stt_[:, 6:7]),
                         reads=[('gv', n4) for n4 in range(4)] + [stall[6], stall[7]], writes=[('vn', s)])
                for c16 in range(16):
                    if cfg.get('dbg') in ('G0', 'G1', 'G1a', 'G1b', 'G1c', 'G1d'):
                        continue
                    pi = c16 % 2
                    g8 = c16 // 2
                    ku = getbank()
                    for c in range(8):
                        S.op('pe', 'matmul', dict(out=banks[ku][:, 0:GT], lhsT=win[:, c, c16 * 128:(c16 + 1) * 128], rhs=hT[:, c, :],
                                                                                start=(c == 0), stop=(c == 7)),
                             reads=hreads + wreads, writes=[('ps', ku)])
                    S.op('act', 'activation', dict(out=gu[pi][:], in_=banks[ku][:, 0:GT], func=AF.Gelu), reads=[('ps', ku)], writes=[('gu', pi)])
                    kg = getbank()
                    for c in range(8):
                        S.op('pe', 'matmul', dict(out=banks[kg][:, 0:GT], lhsT=win[:, c, 4096 + c16 * 128:4096 + (c16 + 1) * 128],
                                                                                rhs=hT[:, c, :], start=(c == 0), stop=(c == 7)),
                             reads=hreads + wreads, writes=[('ps', kg)])
                    S.op('act', 'activation', dict(out=sg[pi][:], in_=banks[kg][:, 0:GT], func=AF.Tanh, scale=0.5), reads=[('ps', kg)], writes=[('sg', pi)])
                    S.op('dve', 'scalar_tensor_tensor', dict(out=sg[pi][:], in0=sg[pi][:], scalar=1.0, in1=banks[kg][:, 0:GT], op0=ALU.add, op1=ALU.mult),
                         reads=[('ps', kg), ('sg', pi)], writes=[('sg', pi)])
                    S.op('pool', 'tensor_tensor', dict(out=up[pi][:], in0=gu[pi][:], in1=sg[pi][:], op=ALU.mult),
                         reads=[('gu', pi), ('sg', pi)], writes=[('up', pi)])
                    ks = getbank()
                    for s in range(NS):
                        S.op('pe', 'matmul', dict(out=banks[ks][:, s * 128:(s + 1) * 128], lhsT=vn[:, s, c16 * 128:(c16 + 1) * 128],
                                                                                rhs=wst[:, g8, :], start=True, stop=True),
                             reads=[('vn', s), 'wst'], writes=[('ps', ks)])
                    S.op('dve', 'scalar_tensor_tensor', dict(
                        out=tz[pi][:].rearrange("p (s x) -> p s x", s=NS), in0=banks[ks][:, 0:GT].rearrange("p (s x) -> p s x", s=NS),
                        scalar=lng[:, c16:c16 + 1], in1=BT[:, c16:c16 + 1, :].to_broadcast([128, NS, 128]), op0=ALU.mult, op1=ALU.add),
                        reads=[('ps', ks), 'BT', 'vecs'], writes=[('tz', pi)])
                    S.op('dve', 'tensor_tensor', dict(out=zT[:, c16, :], in0=tz[pi][:], in1=up[pi][:], op=ALU.mult),
                         reads=[('tz', pi), ('up', pi)], writes=[('zT', c16)])
                for s in range(NS):
                    if cfg.get('dbg') in ('G0', 'G1', 'G1a', 'G2', 'G1b', 'G1c', 'G1d'):
                        continue
                    r0 = tok0 + s * 128
                    oi = ti * NS + s
                    xs, ys = oi % 2, oi % 2
                    S.op('sp', 'dma_start', dict(out=xr[xs][:], in_=src[r0:r0 + 128, :]), writes=[('yo', xs, 0), ('yo', xs, 1)], dma=('xr', xs))
                    for nh in range(2):
                        k = getbank()
                        for c in range(16):
                            S.op('pe', 'matmul', dict(out=banks[k][:], lhsT=zT[:, c, s * 128:(s + 1) * 128],
                                                                               rhs=wout[:, c, nh * 512:(nh + 1) * 512], start=(c == 0), stop=(c == 15)),
                                 reads=[('zT', c), 'wout'], writes=[('ps', k)])
                        S.op('dve', 'tensor_tensor', dict(out=yo[ys][:, nh * 512:(nh + 1) * 512], in0=banks[k][:],
                                                                                      in1=xr[xs][:, nh * 512:(nh + 1) * 512], op=ALU.add),
                             reads=[('ps', k), ('yo', ys, nh)], writes=[('yo', ys, nh)])
                    yres = [('yo', ys, 0), ('yo', ys, 1)]
                    if final:
                        fs = fss[oi % 3]
                        fres = ('fss', oi % 3)
                        S.op('act', 'activation', dict(out=junk[:], in_=yo[ys][:], func=AF.Square, scale=1.0 / 32, accum_out=fs[:, 0:1]),
                             reads=yres, writes=['junk', fres])
                        S.op('act', 'activation', dict(out=fs[:, 1:2], in_=fs[:, 0:1], func=AF.Ln, bias=epsb[:]), reads=[fres, 'epsb'], writes=[fres])
                        S.op('act', 'activation', dict(out=fs[:, 2:3], in_=fs[:, 1:2], func=AF.Exp, scale=-0.5), reads=[fres], writes=[fres])
                        S.op('dve', 'scalar_tensor_tensor', dict(out=yo[ys][:], in0=yo[ys][:], scalar=fs[:, 2:3], in1=fgb[:],
                                                                                 op0=ALU.mult, op1=ALU.mult),
                             reads=yres + [fres, 'fgb'], writes=yres)
                    S.op('pool', 'dma_start', dict(out=dst[r0:r0 + 128, :], in_=yo[ys][:]), reads=yres, dma=('yo', ys))
        S.barrier()
        A.release(mL)

    for l in range(depth):
        j = l // 2
        sd = []
        for st in streams:
            src = st['x'] if l == 0 else st['scr'][(l - 1) % 2]
            dst = st['y'] if l == depth - 1 else st['scr'][l % 2]
            sd.append((st['n_own'], src, dst))
            if l % 2 == 0 and not (cfg.get('dbg') and (st is not streams[0] or cfg.get('dbg')[0] == 'G')):
                mla_layer(st, j, src, dst)
        if l % 2 == 1:
            gmlp_layer(j, sd, final=(l == depth - 1))
    b.peak = A.peak

    keys = S.finalize()
    with ExitStack() as es:
        sems = {k: es.enter_context(nc.semaphore("s%d" % i)) for i, k in enumerate(keys)}
        with nc.Block() as block:
            @block.tensor
            def _(e):
                S.emit('pe', e, sems)

            @block.scalar
            def _(e):
                S.emit('act', e, sems)

            @block.vector
            def _(e):
                S.emit('dve', e, sems)

            @block.gpsimd
            def _(e):
                S.emit('pool', e, sems, final_wait_dma=True)

            @block.sync
            def _(e):
                S.emit('sp', e, sems, final_wait_dma=True)
    return nc, S, b


def rope_table(n):
    inv = (10000.0 ** (-np.arange(0, 64, 2, dtype=np.float32) / np.float32(64))).astype(np.float32)
    ang = np.arange(n, dtype=np.float32)[:, None] * inv[None, :]
    ang = np.concatenate([ang, ang], axis=-1)
    cs = np.stack([np.cos(ang), np.sin(ang)], 0).astype(np.float32)
    cs = np.transpose(cs, (2, 0, 1))
    return np.ascontiguousarray(np.concatenate([cs, cs], axis=0))


def pack_vecs(norm_g, a_q_norm, a_kv_norm, b_ln_g, b_ln_b):
    v = np.zeros((128, 128), np.float32)

    def put(off, arr):
        L, n = arr.shape
        C = n // 128
        v[:, off:off + L * C] = np.transpose(arr.reshape(L, C, 128), (2, 0, 1)).reshape(128, L * C)

    put(0, norm_g)
    put(32, a_q_norm)
    put(38, a_kv_norm)
    put(42, b_ln_g)
    put(74, b_ln_b)
    return v


_CACHE = {}


def run(cfg, per_core_x, weights, cs_tabs):
    key = str([(s['name'], s['n_own'], s['skv'], s.get('exch')) for s in cfg['streams']]) + str(cfg.get('depth', 4))
    if key not in _CACHE:
        _CACHE[key] = build(cfg)
    nc, S, b = _CACHE[key]
    common = {
        "a_w_in": weights["a_w_in"], "a_w_q_up": weights["a_w_q_up"], "a_w_kv_up": weights["a_w_kv_up"], "a_w_out": weights["a_w_out"],
        "b_w_in": weights["b_w_in"], "b_w_out": weights["b_w_out"],
        "b_w_sT": np.ascontiguousarray(np.transpose(weights["b_w_s"], (0, 3, 1, 2)).reshape(2, 128, 1024)),
        "b_b_s": np.ascontiguousarray(weights["b_b_s"].reshape(2, 1024)),
        "vecs": pack_vecs(weights["norm_g"], weights["a_q_norm"], weights["a_kv_norm"], weights["b_ln_g"], weights["b_ln_b"]),
        "final_g": weights["final_g"], "ident": np.eye(128, dtype=np.float32),
    }
    in_maps = []
    ncores = len(per_core_x)
    for c in range(ncores):
        m = dict(common)
        for st in cfg['streams']:
            m["x_" + st['name']] = per_core_x[c][st['name']]
            m["cs_" + st['name']] = cs_tabs[c][st['name']]
        in_maps.append(m)
    res = run_bass_kernel_spmd(nc, in_maps, core_ids=list(range(ncores)))
    return res.results


def kernel(x_prompt, x_sample, norm_g, final_g, a_w_in, a_q_norm, a_kv_norm, a_w_q_up, a_w_kv_up, a_w_out,
           b_w_in, b_ln_g, b_ln_b, b_w_s, b_b_s, b_w_out):
    f = lambda a: np.ascontiguousarray(np.asarray(a, dtype=np.float32))
    weights = dict(norm_g=f(norm_g), final_g=f(final_g), a_w_in=f(a_w_in), a_q_norm=f(a_q_norm), a_kv_norm=f(a_kv_norm),
                   a_w_q_up=f(a_w_q_up), a_w_kv_up=f(a_w_kv_up), a_w_out=f(a_w_out), b_w_in=f(b_w_in), b_ln_g=f(b_ln_g),
                   b_ln_b=f(b_ln_b), b_w_s=f(b_w_s), b_b_s=f(b_b_s), b_w_out=f(b_w_out))
    xp, xs = f(x_prompt), f(x_sample)
    SP, SS = xp.shape[1], xs.shape[1]
    cfg = dict(streams=[dict(name="p", n_own=SP, skv=SP), dict(name="s", n_own=SS, skv=SS)], depth=4)
    tp, ts = rope_table(SP), rope_table(SS)
    per_core_x = [{"p": xp[c], "s": xs[c // 4]} for c in range(NCORES)]
    cs_tabs = [{"p": tp, "s": ts} for c in range(NCORES)]
    res = run(cfg, per_core_x, weights, cs_tabs)
    yp = np.stack([res[c]["y_p"] for c in range(NCORES)], 0)
    q = SS // 4
    ys = np.stack([np.concatenate([res[4 * s + i]["y_s"][i * q:(i + 1) * q] for i in range(4)], 0) for s in range(2)], 0)
    return yp.astype(np.float32), ys.astype(np.float32)
```

```python
import numpy as np
from contextlib import ExitStack
import concourse.bass as bass
import concourse.mybir as mybir
from concourse.bass_utils import run_bass_kernel_spmd

F32 = mybir.dt.float32
BF16 = mybir.dt.bfloat16
AF = mybir.ActivationFunctionType
ALU = mybir.AluOpType

D = 1024
A_IN = 1728
NCORES = 8
QB = 2048
TT = 512
GT = 256
SCALE = 192.0 ** -0.5


class Sched:
    def __init__(self):
        self.ops = []
        self.deps = []
        self.last_w = {}
        self.readers = {}
        self.last_by_class = {}
        self.pending = {}
        self.incs = {}

    def _cls(self, i):
        eng, fn, dma = self.ops[i]
        return ('dma', dma) if dma is not None else ('eng', eng)

    def op(self, engine, method, kw, reads=(), writes=(), dma=None, inc=16):
        fn = (method, kw)
        ps = [r for r in reads if isinstance(r, tuple) and r[0] == 'ps']
        if ps:
            reads = [r for r in reads if r not in ps]
            writes = list(writes) + ps
        i = len(self.ops)
        d = set()
        for r in reads:
            d.update(self.last_w.get(r, {}).values())
        for w in writes:
            d.update(self.last_w.get(w, {}).values())
            d.update(self.readers.get(w, {}).values())
        if engine in self.pending:
            d.update(self.pending.pop(engine))
        self.ops.append((engine, fn, dma))
        self.incs[i] = inc
        c = self._cls(i)
        if engine == 'pe':
            d = {j for j in d if self._cls(j) != ('eng', 'pe')}
        self.deps.append(d)
        for r in reads:
            self.readers.setdefault(r, {})[c] = i
        for w in writes:
            self.last_w[w] = {c: i}
            self.readers[w] = {}
        self.last_by_class[c] = i
        return i

    def barrier(self):
        lasts = set(self.last_by_class.values())
        for e in ('pe', 'act', 'dve', 'pool', 'sp'):
            self.pending[e] = set(lasts) | self.pending.get(e, set())

    def finalize(self):
        n = len(self.ops)
        sig = [False] * n
        for d in self.deps:
            for j in d:
                sig[j] = True
        cnt = {}
        self.sigval = [0] * n
        self.semkey = [None] * n
        for i, (eng, fn, dma) in enumerate(self.ops):
            if dma is not None:
                k = ('dma', dma)
                cnt[k] = cnt.get(k, 0) + self.incs[i]
            elif sig[i]:
                k = ('eng', eng)
                cnt[k] = cnt.get(k, 0) + 1
            else:
                continue
            self.sigval[i] = cnt[k]
            self.semkey[i] = k
        self.final_counts = cnt
        return sorted(cnt.keys(), key=str)

    def emit(self, engine, eh, sems, final_wait_dma=False):
        waited = {}
        for i, (eng, fn, dma) in enumerate(self.ops):
            if eng != engine:
                continue
            need = {}
            for j in self.deps[i]:
                k = self.semkey[j]
                if need.get(k, 0) < self.sigval[j]:
                    need[k] = self.sigval[j]
            for k, v in need.items():
                if waited.get(k, 0) >= v:
                    continue
                eh.wait_ge(sems[k], v)
                waited[k] = v
            ins = getattr(eh, fn[0])(**fn[1])
            k = self.semkey[i]
            if k is not None:
                ins.then_inc(sems[k], self.incs[i] if dma is not None else 1)
        if final_wait_dma:
            for k, v in self.final_counts.items():
                if k[0] == 'dma':
                    eh.wait_ge(sems[k], v)


NAMES = {}


class Arena:
    def __init__(self, nc, base=16512, limit=229344):
        self.nc, self.top, self.n, self.limit, self.peak = nc, base, 0, limit, 0

    def alloc(self, name, shape, dt):
        nb = int(np.prod(shape[1:])) * (2 if dt == BF16 else 4)
        off = (self.top + 63) // 64 * 64
        assert off + nb <= self.limit, (name, off, nb)
        self.top = off + nb
        self.peak = max(self.peak, self.top)
        self.n += 1
        h = self.nc.alloc_sbuf_tensor_at(f"{name}_{self.n}", list(shape), dt, offset=off)
        NAMES.setdefault(name, []).append(h.name)
        return h

    def mark(self):
        return self.top

    def release(self, m):
        self.top = m


class B:
    pass


def build(cfg):
    nc = bass.Bass("TRN2", target_bir_lowering=False)
    S = Sched()
    A = Arena(nc)
    b = B()
    streams = cfg['streams']
    depth = cfg.get('depth', 4)

    def din(name, shape, dt=F32):
        return nc.dram_tensor(name, list(shape), dt, kind="ExternalInput").ap()

    def dscr(name, shape, dt):
        return nc.dram_tensor(name, list(shape), dt, kind="Internal").ap()

    for st in streams:
        n = st['n_own']
        st['x'] = din("x_" + st['name'], [n, D])
        st['cs'] = din("cs_" + st['name'], [128, 2, n])
        st['y'] = nc.dram_tensor("y_" + st['name'], [n, D], F32, kind="ExternalOutput").ap()
        st['scr'] = [dscr(f"xs{i}_" + st['name'], [n, D], F32) for i in range(2)]
    for st in streams:
        if st.get('exch'):
            n = st['n_own']
            st['xin'] = [dscr(f"xin{j}_" + st['name'], [128, 2 * n + n // 2], BF16) for j in range(2)]
            st['xout'] = [dscr(f"xout{j}_" + st['name'], [(st['skv'] // n) * 128, 2 * n + n // 2], BF16) for j in range(2)]
    a_w_in = din("a_w_in", [2, D, A_IN])
    a_w_q_up = din("a_w_q_up", [2, 384, 1536])
    a_w_kv_up = din("a_w_kv_up", [2, 256, 2048])
    a_w_out = din("a_w_out", [2, D, D])
    b_w_in = din("b_w_in", [2, D, 6144])
    b_w_out = din("b_w_out", [2, 2048, D])
    b_w_sT = din("b_w_sT", [2, 128, 1024])
    b_b_s = din("b_b_s", [2, 1024])
    vecs_d = din("vecs", [128, 128])
    final_g = din("final_g", [D])
    ident_d = din("ident", [128, 128])
    NL = 2
    wA = [dscr(f"wA{j}", [128, 8 * 512], BF16) for j in range(NL)]
    wQ = [dscr(f"wQ{j}", [128, 8 * 1408], BF16) for j in range(NL)]
    wBq = [dscr(f"wBq{j}", [128, 3 * 3072], BF16) for j in range(NL)]
    wBkv = [dscr(f"wBkv{j}", [128, 2 * 2048], BF16) for j in range(NL)]
    wC = [dscr(f"wC{j}", [128, 8 * 1024], BF16) for j in range(NL)]
    wGin = [dscr(f"wGin{j}", [128, 8 * 6144], BF16) for j in range(NL)]
    wGout = [dscr(f"wGout{j}", [128, 16 * 1024], BF16) for j in range(NL)]
    wsT = [dscr(f"wsT{j}", [128, 1024], BF16) for j in range(NL)]
    BTd = [dscr(f"BT{j}", [128, 2048], F32) for j in range(NL)]

    banks = [nc.alloc_psum_tensor(f"bank{i}", [128, 512], F32) for i in range(8)]
    b.bank_rr = 0

    def getbank(pool=(0, 1, 2, 3, 4, 5, 6, 7)):
        k = pool[b.bank_rr % len(pool)]
        b.bank_rr += 1
        return k

    def bankbf(k):
        return banks[k][:].bitcast(BF16)

    ident = A.alloc("ident", [128, 128], BF16)
    ones = A.alloc("ones", [128, 128], BF16)
    onesf = A.alloc("onesf", [128, 128], F32)
    epsb = A.alloc("epsb", [128, 1], F32)
    eps5 = A.alloc("eps5", [128, 1], F32)
    vecs = A.alloc("vecs", [128, 128], F32)
    nvecs = A.alloc("nvecs", [128, 128], F32)
    fgb = A.alloc("fgb", [128, D], F32)
    junk = A.alloc("junk", [128, D], BF16)
    VNG, VQN, VKVN, VLNG, VLNB = 0, 32, 38, 42, 74

    m0 = A.mark()
    idf = A.alloc("idf", [128, 128], F32)
    S.op('sp', 'dma_start', dict(out=idf[:], in_=ident_d), writes=['idf'], dma=('stf', 0))
    S.op('sp', 'dma_start', dict(out=vecs[:], in_=vecs_d), writes=['vecs'], dma=('stf', 1))
    S.op('sp', 'dma_start', dict(out=fgb[:], in_=final_g.partition_broadcast(128)), writes=['fgb'], dma=('stf', 2))
    S.op('dve', 'tensor_copy', dict(out=ident[:], in_=idf[:]), reads=['idf'], writes=['ident'])
    S.op('pool', 'memset', dict(ap=ones[:], constant=1.0), writes=['ones'])
    S.op('pool', 'memset', dict(ap=onesf[:], constant=1.0), writes=['onesf'])
    S.op('pool', 'memset', dict(ap=epsb[:], constant=1e-6), writes=['epsb'])
    S.op('pool', 'memset', dict(ap=eps5[:], constant=1e-5), writes=['eps5'])
    S.op('dve', 'tensor_scalar', dict(out=nvecs[:], in0=vecs[:], scalar1=-1.0, scalar2=None, op0=ALU.mult),
         reads=['vecs'], writes=['nvecs'])

    S.barrier()
    stf = [A.alloc("stf", [128, 2048], F32) for _ in range(3)]
    stb = [A.alloc("stb", [128, 3072], BF16) for _ in range(3)]
    b.pp = 0

    def piece(loads, convs, dst, ncols_out):
        i = b.pp
        b.pp += 1
        s = i % 3
        eng = 'dve' if i % 2 == 0 else 'act'
        for (c0, ncl, ap) in loads:
            S.op('sp', 'dma_start', dict(out=stf[s][:, c0:c0 + ncl], in_=ap),
                 writes=[('stf', s)], dma=('stf', s))
        for (ov, iv, sc) in convs:
            o_ap, i_ap = ov(stb[s]), iv(stf[s])
            if sc is None:
                if eng == 'dve':
                    fn = 'tensor_copy', dict(out=o_ap, in_=i_ap)
                else:
                    fn = 'activation', dict(out=o_ap, in_=i_ap, func=AF.Copy)
            else:
                if eng == 'dve':
                    fn = 'tensor_scalar', dict(out=o_ap, in0=i_ap, scalar1=sc, scalar2=None, op0=ALU.mult)
                else:
                    fn = 'activation', dict(out=o_ap, in_=i_ap, func=AF.Copy, scale=sc)
            S.op(eng, fn[0], fn[1], reads=[('stf', s), 'vecs', 'nvecs'], writes=[('stb', s)])
        S.op('pool', 'dma_start', dict(out=dst, in_=stb[s][:, 0:ncols_out]), reads=[('stb', s)], dma=('stbo', s))

    def sl(c0, c1):
        return lambda t: t[:, c0:c1]

    for j in range(NL):
        if cfg.get('dbg') == 'nopiece':
            break
        lyr = 2 * j
        for c in range(8):
            g = vecs[:, VNG + lyr * 8 + c:VNG + lyr * 8 + c + 1]
            ngm = nvecs[:, VNG + lyr * 8 + c:VNG + lyr * 8 + c + 1]
            rows = slice(c * 128, (c + 1) * 128)
            piece([(0, 320, a_w_in[j, rows, 384:704])],
                  [(sl(0, 256), sl(0, 256), g), (sl(256, 320), sl(256, 320), g), (sl(320, 384), sl(256, 320), g),
                   (sl(384, 416), sl(288, 320), ngm), (sl(416, 448), sl(256, 288), g),
                   (sl(448, 480), sl(288, 320), ngm), (sl(480, 512), sl(256, 288), g)],
                  wA[j][:, c * 512:(c + 1) * 512], 512)
            piece([(0, 384, a_w_in[j, rows, 0:384]), (384, 1024, a_w_in[j, rows, 704:1728])],
                  [(sl(0, 1408), sl(0, 1408), g)], wQ[j][:, c * 1408:(c + 1) * 1408], 1408)
        for c in range(3):
            g = vecs[:, VQN + j * 3 + c:VQN + j * 3 + c + 1]
            ngm = nvecs[:, VQN + j * 3 + c:VQN + j * 3 + c + 1]
            iv = lambda a, bb: (lambda t: t[:, 0:1536].rearrange("p (h x) -> p h x", h=8)[:, :, a:bb])
            ov = lambda a, bb: (lambda t: t[:, 0:3072].rearrange("p (h x) -> p h x", h=8)[:, :, a:bb])
            piece([(0, 1536, a_w_q_up[j, c * 128:(c + 1) * 128, :])],
                  [(ov(0, 128), iv(0, 128), g), (ov(128, 192), iv(128, 192), g), (ov(192, 256), iv(128, 192), g),
                   (ov(256, 288), iv(160, 192), ngm), (ov(288, 320), iv(128, 160), g),
                   (ov(320, 352), iv(160, 192), ngm), (ov(352, 384), iv(128, 160), g)],
                  wBq[j][:, c * 3072:(c + 1) * 3072], 3072)
        for c in range(2):
            g = vecs[:, VKVN + j * 2 + c:VKVN + j * 2 + c + 1]
            piece([(0, 2048, a_w_kv_up[j, c * 128:(c + 1) * 128, :])], [(sl(0, 2048), sl(0, 2048), g)],
                  wBkv[j][:, c * 2048:(c + 1) * 2048], 2048)
        for c in range(0, 8, 2):
            piece([(0, 1024, a_w_out[j, c * 128:(c + 1) * 128, :]), (1024, 1024, a_w_out[j, (c + 1) * 128:(c + 2) * 128, :])],
                  [(sl(0, 2048), sl(0, 2048), None)], wC[j][:, c * 1024:(c + 2) * 1024], 2048)
        lyr = 2 * j + 1
        for c in range(8):
            g = vecs[:, VNG + lyr * 8 + c:VNG + lyr * 8 + c + 1]
            for n3 in range(3):
                piece([(0, 2048, b_w_in[j, c * 128:(c + 1) * 128, n3 * 2048:(n3 + 1) * 2048])],
                      [(sl(0, 2048), sl(0, 2048), g)], wGin[j][:, c * 6144 + n3 * 2048:c * 6144 + (n3 + 1) * 2048], 2048)
        for c in range(0, 16, 2):
            piece([(0, 1024, b_w_out[j, c * 128:(c + 1) * 128, :]), (1024, 1024, b_w_out[j, (c + 1) * 128:(c + 2) * 128, :])],
                  [(sl(0, 2048), sl(0, 2048), 0.5)], wGout[j][:, c * 1024:(c + 2) * 1024], 2048)
        piece([(0, 1024, b_w_sT[j])], [(sl(0, 1024), sl(0, 1024), None)], wsT[j][:, :], 1024)
        if cfg.get('dbg') == 'nobt':
            continue
        wsf = A.alloc("wsf", [128, 1024], F32)
        bsb = A.alloc("bsb", [128, 1024], F32)
        btt = A.alloc("btt", [128, 2048], F32)
        S.op('sp', 'dma_start', dict(out=wsf[:], in_=b_w_sT[j]), writes=['wsf'], dma=('xr', 0))
        S.op('sp', 'dma_start', dict(out=bsb[:], in_=b_b_s[j].partition_broadcast(128)), writes=['bsb'], dma=('xr', 1))
        whi = A.alloc("whi", [128, 1024], BF16)
        wlo = A.alloc("wlo", [128, 1024], BF16)
        S.op('dve', 'tensor_copy', dict(out=whi[:], in_=wsf[:]), reads=['wsf'], writes=['whi'])
        S.op('dve', 'tensor_tensor', dict(out=wlo[:], in0=wsf[:], in1=whi[:], op=ALU.subtract), reads=['wsf', 'whi'], writes=['wlo'])
        for hlf in range(2):
            k = getbank()
            S.op('pe', 'matmul', dict(out=banks[k][:], lhsT=ones[:], rhs=whi[:, hlf * 512:(hlf + 1) * 512], start=True, stop=False),
                 reads=['ones', 'whi'], writes=[('ps', k)])
            S.op('pe', 'matmul', dict(out=banks[k][:], lhsT=ones[:], rhs=wlo[:, hlf * 512:(hlf + 1) * 512], start=False, stop=True),
                 reads=['ones', 'wlo'], writes=[('ps', k)])
            for gg in range(4):
                g8 = hlf * 4 + gg
                for cc in range(2):
                    c16 = g8 * 2 + cc
                    S.op('dve', 'scalar_tensor_tensor', dict(
                        out=btt[:, c16 * 128:(c16 + 1) * 128], in0=banks[k][:, gg * 128:(gg + 1) * 128],
                        scalar=vecs[:, VLNB + j * 16 + c16:VLNB + j * 16 + c16 + 1], in1=bsb[:, g8 * 128:(g8 + 1) * 128],
                        op0=ALU.mult, op1=ALU.add), reads=[('ps', k), 'bsb', 'vecs'], writes=['btt'])
        S.op('pool', 'dma_start', dict(out=BTd[j][:, :], in_=btt[:]), reads=['btt'], dma=('stbo', 0))
        S.barrier()
        A.release(A.mark() - 0)
    S.barrier()
    A.release(m0)

    b.xc = 0

    def front(fb, src, tok0, ntok, hT, hTres):
        for s in range(ntok // 128):
            i = b.xc
            b.xc += 1
            xs, hs = i % fb.nx, i % 2
            xt, hb, ss = fb.xt[xs], fb.hb[hs], fb.ss[xs]
            r0 = tok0 + s * 128
            S.op('sp', 'dma_start', dict(out=xt[:], in_=src[r0:r0 + 128, :]), writes=[('xt', xs)], dma=('xt', xs))
            S.op('act', 'activation', dict(out=junk[:], in_=xt[:], func=AF.Square, scale=1.0 / 32, accum_out=ss[:, 0:1]),
                 reads=[('xt', xs)], writes=['junk', ('ss', xs)])
            S.op('act', 'activation', dict(out=ss[:, 1:2], in_=ss[:, 0:1], func=AF.Ln, bias=epsb[:]),
                 reads=[('ss', xs), 'epsb'], writes=[('ss', xs)])
            S.op('act', 'activation', dict(out=ss[:, 2:3], in_=ss[:, 1:2], func=AF.Exp, scale=-0.5),
                 reads=[('ss', xs)], writes=[('ss', xs)])
            S.op('dve', 'tensor_scalar', dict(out=hb[:], in0=xt[:], scalar1=ss[:, 2:3], scalar2=None, op0=ALU.mult),
                 reads=[('xt', xs), ('ss', xs)], writes=[('hb', hs)])
            k = getbank()
            tv = bankbf(k).rearrange("p (c t) -> p c t", c=8)
            for c in range(8):
                S.op('pe', 'transpose', dict(out=tv[:, c, :], in_=hb[:, c * 128:(c + 1) * 128], identity=ident[:]),
                     reads=[('hb', hs), 'ident'], writes=[('ps', k)])
            S.op('dve', 'tensor_copy', dict(out=hT[:, :, s * 128:(s + 1) * 128], in_=tv),
                 reads=[('ps', k)], writes=[hTres + (s,)])

    def alloc_front(ntok, nx=3):
        fb = B()
        fb.nx = nx
        fb.xt = [A.alloc("xt", [128, D], F32) for _ in range(nx)]
        fb.hb = [A.alloc("hb", [128, D], BF16) for _ in range(2)]
        fb.ss = [A.alloc("ss", [128, 4], F32) for _ in range(3)]
        fb.hT = [A.alloc("hT", [128, 8, ntok], BF16) for _ in range(2)]
        fb.n = 0
        return fb

    def rstd_bc(k, nfeat, out_ap, res_w):
        S.op('act', 'activation', dict(out=out_ap, in_=banks[k][:], func=AF.Ln, scale=1.0 / nfeat, bias=epsb[:]),
             reads=[('ps', k), 'epsb'], writes=[res_w])
        S.op('act', 'activation', dict(out=out_ap, in_=out_ap, func=AF.Exp, scale=-0.5), reads=[res_w], writes=[res_w])

    def load_w(dst, src, res, key):
        S.op('sp', 'dma_start', dict(out=dst, in_=src), writes=[res], dma=key)

    def mla_layer(st, j, src, dst):
        n_own, skv = st['n_own'], st['skv']
        exch = bool(st.get('exch'))
        QBl = min(QB, n_own)
        mL = A.mark()
        kvT = A.alloc("kvT", [128, 2, skv], BF16)
        krT = A.alloc("krT", [128, skv // 2], BF16)
        wreg = A.alloc("wreg", [128, 13312], BF16)
        mP = A.mark()
        kvT_all, krT_all = kvT, krT
        if exch:
            kvT = A.alloc("kvTo", [128, 2, n_own], BF16)
            krT = A.alloc("krTo", [128, n_own // 2], BF16)
        RK, RR = ('kvTo', 'krTo') if exch else ('kvT', 'krT')
        fb = alloc_front(TT)
        wa = wreg[:, 0:4096].rearrange("p (c n) -> p c n", c=8)
        load_w(wreg[:, 0:4096], wA[j][:, :], 'wreg', 'wreg')
        cst = [A.alloc("cst", [128, 2, TT], F32) for _ in range(2)]
        sqb = [A.alloc("sqb", [128, TT], BF16) for _ in range(2)]
        raw = [A.alloc("raw", [128, TT], BF16) for _ in range(2)]
        rbc = A.alloc("rbc", [128, TT], F32)
        t1 = A.alloc("t1", [128, TT], F32)
        t2 = A.alloc("t2", [128, TT], F32)
        for t in range(n_own // TT):
            tok0 = t * TT
            hT = fb.hT[t % 2]
            hres = ('hT', t % 2)
            front(fb, src, tok0, TT, hT, hres)
            hreads = [hres + (s,) for s in range(4)]
            if cfg.get('dbg') == 'A1':
                continue
            cs = cst[t % 2]
            if cfg.get('dbg') != 'A2b':
                S.op('sp', 'dma_start', dict(out=cs[:], in_=st['cs'][:, :, tok0:tok0 + TT]),
                     writes=[('cst', t % 2)], dma=('cq', t % 2))
            kb = []
            for m in range(4):
                k = getbank()
                kb.append(k)
                for c in range(8):
                    S.op('pe', 'matmul', dict(out=banks[k][:], lhsT=wa[:, c, m * 128:(m + 1) * 128], rhs=hT[:, c, :],
                                                                       start=(c == 0), stop=(c == 7)),
                         reads=hreads + ['wreg'], writes=[('ps', k)])
            for m in range(2):
                S.op('act', 'activation', dict(out=sqb[m][:], in_=banks[kb[m]][:], func=AF.Square),
                     reads=[('ps', kb[m])], writes=[('sqb', m)])
                S.op('dve', 'tensor_copy', dict(out=raw[m][:], in_=banks[kb[m]][:]), reads=[('ps', kb[m])], writes=[('raw', m)])
            kd = getbank()
            for m in range(2):
                if cfg.get('dbg') == 'A2c':
                    continue
                S.op('pe', 'matmul', dict(out=banks[kd][:], lhsT=ones[:], rhs=sqb[m][:], start=(m == 0), stop=(m == 1)),
                     reads=[('sqb', m), 'ones'], writes=[('ps', kd)])
            if cfg.get('dbg') in ('A2', 'A2b', 'A2c'):
                continue
            rstd_bc(kd, 256, rbc[:], 'rbc')
            for m in range(2):
                S.op('dve', 'tensor_tensor', dict(out=kvT[:, m, tok0:tok0 + TT], in0=raw[m][:], in1=rbc[:], op=ALU.mult),
                     reads=[('raw', m), 'rbc'], writes=[(RK, t)])
            S.op('dve', 'tensor_tensor', dict(out=t1[:], in0=banks[kb[2]][:], in1=cs[:, 0, :], op=ALU.mult),
                 reads=[('ps', kb[2]), ('cst', t % 2)], writes=['t1'])
            S.op('dve', 'tensor_tensor', dict(out=t2[:], in0=banks[kb[3]][:], in1=cs[:, 1, :], op=ALU.mult),
                 reads=[('ps', kb[3]), ('cst', t % 2)], writes=['t2'])
            for hf in range(2):
                if cfg.get('dbg') == 'A3':
                    continue
                S.op('pool', 'tensor_tensor', dict(
                    out=krT[64 * hf:64 * hf + 64, 2 * t * 128:(2 * t + 2) * 128].rearrange("p (i x) -> p i x", i=2),
                    in0=t1[64 * hf:64 * hf + 64, :].rearrange("p (i two x) -> p i two x", i=2, two=2)[:, :, hf, :],
                    in1=t2[64 * hf:64 * hf + 64, :].rearrange("p (i two x) -> p i two x", i=2, two=2)[:, :, hf, :],
                    op=ALU.add), reads=['t1', 't2'], writes=[(RR, t, hf)])
        if exch:
            nt = n_own // TT
            allw = [(RK, t) for t in range(nt)] + [(RR, t, hf) for t in range(nt) for hf in range(2)]
            xin, xout = st['xin'][j], st['xout'][j]
            W = 2 * n_own + n_own // 2
            S.op('pool', 'dma_start', dict(out=xin[:, 0:2 * n_own].rearrange("p (c n) -> p c n", c=2), in_=kvT[:]), reads=allw, dma='xi')
            S.op('pool', 'dma_start', dict(out=xin[:, 2 * n_own:W], in_=krT[:]), reads=allw, writes=['xin'], dma='xi')
            S.barrier()
            S.op('pool', 'collective_compute', dict(kind="AllGather", op=ALU.bypass, replica_groups=[list(range(NCORES))],
                                                   ins=[xin[:, :]], outs=[xout[:, :]]), reads=['xin'], writes=['xout'], dma=('cc', j), inc=1)
            S.barrier()
            kvT, krT = kvT_all, krT_all
            for r in range(skv // n_own):
                S.op('sp', 'dma_start', dict(out=kvT[:, :, r * n_own:(r + 1) * n_own],
                                             in_=xout[r * 128:(r + 1) * 128, 0:2 * n_own].rearrange("p (c n) -> p c n", c=2)),
                     reads=['xout'], writes=[('kvT', r * nt + t) for t in range(nt)], dma='xo')
                S.op('sp', 'dma_start', dict(out=krT[:, r * (n_own // 2):(r + 1) * (n_own // 2)], in_=xout[r * 128:(r + 1) * 128, 2 * n_own:W]),
                     reads=['xout'], writes=[('krT', r * nt + t, hf) for t in range(nt) for hf in range(2)], dma='xo')
        S.barrier()
        A.release(mP)
        if cfg.get('dbg') in ('A', 'A1', 'A2', 'A3', 'A2b', 'A2c'):
            return
        for qb in range(n_own // QBl):
            q0 = qb * QBl
            mQ = A.mark()
            qlT = A.alloc("qlT", [128, 3, QBl], BF16)
            GT_ = A.alloc("GT", [128, 8, QBl], BF16)
            mP = A.mark()
            fb = alloc_front(TT)
            wq = wreg[:, 0:11264].rearrange("p (c n) -> p c n", c=8)
            load_w(wreg[:, 0:11264], wQ[j][:, :], 'wreg', 'wreg')
            sqb = [A.alloc("sqb", [128, TT], BF16) for _ in range(3)]
            raw = [A.alloc("raw", [128, TT], BF16) for _ in range(3)]
            rbc = A.alloc("rbc", [128, TT], F32)
            for t in range(QBl // TT):
                tok0 = q0 + t * TT
                l0 = t * TT
                hT = fb.hT[t % 2]
                hres = ('hT', t % 2)
                front(fb, src, tok0, TT, hT, hres)
                hreads = [hres + (s,) for s in range(4)]
                kb = []
                for m in range(3):
                    k = getbank()
                    kb.append(k)
                    for c in range(8):
                        S.op('pe', 'matmul', dict(out=banks[k][:], lhsT=wq[:, c, m * 128:(m + 1) * 128], rhs=hT[:, c, :],
                                                                           start=(c == 0), stop=(c == 7)),
                             reads=hreads + ['wreg'], writes=[('ps', k)])
                    S.op('act', 'activation', dict(out=sqb[m][:], in_=banks[k][:], func=AF.Square),
                         reads=[('ps', k)], writes=[('sqb', m)])
                    S.op('dve', 'tensor_copy', dict(out=raw[m][:], in_=banks[k][:]), reads=[('ps', k)], writes=[('raw', m)])
                kd = getbank()
                for m in range(3):
                    S.op('pe', 'matmul', dict(out=banks[kd][:], lhsT=ones[:], rhs=sqb[m][:], start=(m == 0), stop=(m == 2)),
                         reads=[('sqb', m), 'ones'], writes=[('ps', kd)])
                rstd_bc(kd, 384, rbc[:], 'rbc')
                for m in range(3):
                    S.op('dve', 'tensor_tensor', dict(out=qlT[:, m, l0:l0 + TT], in0=raw[m][:], in1=rbc[:], op=ALU.mult),
                         reads=[('raw', m), 'rbc'], writes=[('qlT', t)])
                for m in range(8):
                    k = getbank()
                    for c in range(8):
                        S.op('pe', 'matmul', dict(out=banks[k][:], lhsT=wq[:, c, 384 + m * 128:384 + (m + 1) * 128],
                                                                           rhs=hT[:, c, :], start=(c == 0), stop=(c == 7)),
                             reads=hreads + ['wreg'], writes=[('ps', k)])
                    S.op('act', 'activation', dict(out=GT_[:, m, l0:l0 + TT], in_=banks[k][:], func=AF.Silu),
                         reads=[('ps', k)], writes=[('GT', m, t)])
            S.barrier()
            A.release(mP)
            if cfg.get('dbg') == 'Q':
                return
            mP = A.mark()
            wbq = wreg[:, 0:9216].rearrange("p (c h x) -> p c h x", c=3, h=8)
            wbkv = wreg[:, 9216:13312].rearrange("p (c n) -> p c n", c=2)
            load_w(wreg[:, 0:9216], wBq[j][:, :], 'wreg', 'wreg')
            load_w(wreg[:, 9216:13312], wBkv[j][:, :], 'wreg2', 'wreg2')
            Kh = A.alloc("Kh", [128, skv], BF16)
            Vh = A.alloc("Vh", [128, skv // 128, 128], BF16)
            qn = A.alloc("qn", [128, QBl], BF16)
            qrA = A.alloc("qrA", [128, QBl], BF16)
            qrB = A.alloc("qrB", [128, QBl], BF16)
            NPT = 12
            PT = [A.alloc("PT", [128, TT], BF16) for _ in range(NPT)]
            cq = [A.alloc("cq", [128, 2, TT], F32) for _ in range(2)]
            t1 = A.alloc("t1", [128, TT], F32)
            t2 = A.alloc("t2", [128, TT], F32)
            ptmp = [A.alloc("ptmp", [128, TT], BF16) for _ in range(4)]
            acc1 = A.alloc("acc1", [128, TT], F32)
            acc2 = A.alloc("acc2", [128, TT], F32)
            ahi = A.alloc("ahi", [128, TT], BF16)
            alo = A.alloc("alo", [128, TT], BF16)
            rden = A.alloc("rden", [128, TT], F32)
            nqt = QBl // TT
            S.op('pool', 'memset', dict(ap=qrA[:], constant=0.0), writes=[('qrA', qt) for qt in range(nqt)])
            S.op('pool', 'memset', dict(ap=qrB[:], constant=0.0), writes=[('qrB', qt) for qt in range(nqt)])
            b.cq = 0
            P4 = (0, 1, 2, 3, 5)
            nkt = skv // 128
            b.pt = 0
            b.ev = 0
            for h in range(8):
                for kt in range(skv // TT):
                    k = getbank(P4)
                    for c in range(2):
                        S.op('pe', 'matmul', dict(out=banks[k][:], lhsT=wbkv[:, c, h * 256:h * 256 + 128],
                                                                           rhs=kvT[:, c, kt * TT:(kt + 1) * TT], start=(c == 0), stop=(c == 1)),
                             reads=['wreg2', ('kvT', kt)], writes=[('ps', k)])
                    b.ev += 1
                    if b.ev % 2:
                        S.op('dve', 'tensor_copy', dict(out=Kh[:, kt * TT:(kt + 1) * TT], in_=banks[k][:]),
                             reads=[('ps', k)], writes=[('Kh', kt)])
                    else:
                        S.op('act', 'activation', dict(out=Kh[:, kt * TT:(kt + 1) * TT], in_=banks[k][:], func=AF.Copy),
                             reads=[('ps', k)], writes=[('Kh', kt)])
                for g4 in range(nkt // 4):
                    k = getbank(P4)
                    for i4 in range(4):
                        jt = g4 * 4 + i4
                        for c in range(2):
                            S.op('pe', 'matmul', dict(
                                out=banks[k][:, i4 * 128:(i4 + 1) * 128], lhsT=kvT[:, c, jt * 128:(jt + 1) * 128],
                                rhs=wbkv[:, c, h * 256 + 128:h * 256 + 256], start=(c == 0), stop=(c == 1)),
                                reads=['wreg2', ('kvT', jt // 4)], writes=[('ps', k)])
                    b.ev += 1
                    if b.ev % 2:
                        S.op('dve', 'tensor_copy', dict(out=Vh[:, g4 * 4:(g4 + 1) * 4, :],
                                                                        in_=banks[k][:].rearrange("p (i x) -> p i x", i=4)),
                             reads=[('ps', k)], writes=[('Vh', g4)])
                    else:
                        S.op('act', 'activation', dict(out=Vh[:, g4 * 4:(g4 + 1) * 4, :],
                                                                       in_=banks[k][:].rearrange("p (i x) -> p i x", i=4), func=AF.Copy),
                             reads=[('ps', k)], writes=[('Vh', g4)])
                for qt in range(QBl // TT):
                    l0 = qt * TT
                    kk = []
                    for part in range(3):
                        k = getbank(P4)
                        kk.append(k)
                        for c in range(3):
                            S.op('pe', 'matmul', dict(
                                out=banks[k][:], lhsT=wbq[:, c, h, part * 128:(part + 1) * 128], rhs=qlT[:, c, l0:l0 + TT],
                                start=(c == 0), stop=(c == 2)), reads=['wreg', ('qlT', qt)], writes=[('ps', k)])
                    S.op('act', 'activation', dict(out=qn[:, l0:l0 + TT], in_=banks[kk[0]][:], func=AF.Copy),
                         reads=[('ps', kk[0])], writes=[('qn', qt)])
                    ci = b.cq % 2
                    b.cq += 1
                    S.op('sp', 'dma_start', dict(out=cq[ci][:], in_=st['cs'][:, :, q0 + l0:q0 + l0 + TT]), writes=[('cq', ci)], dma=('cq', ci))
                    S.op('dve', 'tensor_tensor', dict(out=t1[:], in0=banks[kk[1]][:], in1=cq[ci][:, 0, :], op=ALU.mult),
                         reads=[('ps', kk[1]), ('cq', ci)], writes=['t1'])
                    S.op('dve', 'tensor_tensor', dict(out=t2[:], in0=banks[kk[2]][:], in1=cq[ci][:, 1, :], op=ALU.mult),
                         reads=[('ps', kk[2]), ('cq', ci)], writes=['t2'])
                    S.op('pool', 'tensor_tensor', dict(out=qrA[0:64, l0:l0 + TT], in0=t1[0:64, :], in1=t2[0:64, :], op=ALU.add),
                         reads=['t1', 't2'], writes=[('qrA', qt)])
                    S.op('pool', 'tensor_tensor', dict(out=qrB[64:128, l0:l0 + TT], in0=t1[64:128, :], in1=t2[64:128, :], op=ALU.add),
                         reads=['t1', 't2'], writes=[('qrB', qt)])
                for qt in range(QBl // TT):
                    l0 = qt * TT
                    ko = 4 + (2 * (h * 4 + qt)) % 4
                    kd = 7
                    pend = []

                    first = {'dve': True, 'pool': True}
                    held = []

                    def pv(item, last):
                        jt, sl_ = item
                        S.op('pe', 'matmul', dict(out=banks[ko][:], lhsT=Vh[:, jt, :], rhs=PT[sl_][:], start=(jt == 0), stop=last),
                             reads=[('Vh', jt // 4), ('PT', sl_)], writes=[('ps', ko)])
                        if jt % 2 == 0:
                            held.append(sl_)
                            return
                        sl0 = held.pop()
                        pr = jt // 2
                        eng, acc, res = ('pool', acc2, 'acc2') if pr % 2 == 1 else ('dve', acc1, 'acc1')
                        tb = ptmp[pr % 4]
                        tres = ('ptmp', pr % 4)
                        S.op(eng, 'tensor_tensor', dict(out=tb[:], in0=PT[sl0][:], in1=PT[sl_][:], op=ALU.add),
                             reads=[('PT', sl0), ('PT', sl_)], writes=[tres])
                        if first[eng]:
                            first[eng] = False
                            S.op(eng, 'tensor_copy', dict(out=acc[:], in_=tb[:]), reads=[tres], writes=[res])
                        else:
                            S.op(eng, 'tensor_tensor', dict(out=acc[:], in0=acc[:], in1=tb[:], op=ALU.add), reads=[tres, res], writes=[res])

                    for jt in range(nkt):
                        k = getbank(P4)
                        hf = jt % 2
                        S.op('pe', 'matmul', dict(out=banks[k][:], lhsT=Kh[:, jt * 128:(jt + 1) * 128], rhs=qn[:, l0:l0 + TT],
                                                  start=True, stop=False),
                             reads=[('Kh', jt // 4), ('qn', qt)], writes=[('ps', k)])
                        qz, qres = (qrA, 'qrA') if hf == 0 else (qrB, 'qrB')
                        S.op('pe', 'matmul', dict(out=banks[k][:], lhsT=krT[:, (jt // 2) * 128:(jt // 2 + 1) * 128],
                                                  rhs=qz[:, l0:l0 + TT], start=False, stop=True),
                             reads=[('krT', jt // 4, 0), ('krT', jt // 4, 1), (qres, qt)], writes=[('ps', k)])
                        sl_ = b.pt % NPT
                        b.pt += 1
                        S.op('act', 'activation', dict(out=PT[sl_][:], in_=banks[k][:], func=AF.Exp, scale=SCALE),
                             reads=[('ps', k)], writes=[('PT', sl_)])
                        pend.append((jt, sl_))
                        if len(pend) > 3:
                            pv(pend.pop(0), False)
                    while pend:
                        it = pend.pop(0)
                        pv(it, len(pend) == 0)
                    S.op('dve', 'tensor_tensor', dict(out=acc1[:], in0=acc1[:], in1=acc2[:], op=ALU.add), reads=['acc1', 'acc2'], writes=['acc1'])
                    S.op('dve', 'tensor_copy', dict(out=ahi[:], in_=acc1[:]), reads=['acc1'], writes=['ahi'])
                    S.op('dve', 'tensor_tensor', dict(out=alo[:], in0=acc1[:], in1=ahi[:], op=ALU.subtract), reads=['acc1', 'ahi'], writes=['alo'])
                    S.op('pe', 'matmul', dict(out=banks[kd][:], lhsT=ones[:], rhs=ahi[:], start=True, stop=False), reads=['ones', 'ahi'], writes=[('ps', kd)])
                    S.op('pe', 'matmul', dict(out=banks[kd][:], lhsT=ones[:], rhs=alo[:], start=False, stop=True), reads=['ones', 'alo'], writes=[('ps', kd)])
                    S.op('dve', 'reciprocal', dict(out=rden[:], in_=banks[kd][:]), reads=[('ps', kd)], writes=['rden'])
                    S.op('dve', 'tensor_tensor', dict(out=rden[:], in0=banks[ko][:], in1=rden[:], op=ALU.mult),
                         reads=[('ps', ko), 'rden'], writes=['rden'])
                    S.op('pool', 'tensor_tensor', dict(out=GT_[:, h, l0:l0 + TT], in0=GT_[:, h, l0:l0 + TT], in1=rden[:], op=ALU.mult),
                         reads=['rden', ('GT', h, qt)], writes=[('GT', h, qt)])
            S.barrier()
            A.release(mP)
            if cfg.get('dbg') == 'B':
                return
            mP = A.mark()
            wc = wreg[:, 0:8192].rearrange("p (c n) -> p c n", c=8)
            load_w(wreg[:, 0:8192], wC[j][:, :], 'wreg', 'wreg')
            xr = [A.alloc("xr", [128, D], F32) for _ in range(3)]
            yo = [A.alloc("yo", [128, D], F32) for _ in range(2)]
            for s in range(QBl // 128):
                r0 = q0 + s * 128
                xs, ys = s % 3, s % 2
                S.op('sp', 'dma_start', dict(out=xr[xs][:], in_=src[r0:r0 + 128, :]), writes=[('xr', xs)], dma=('xr', xs))
                for nh in range(2):
                    k = getbank()
                    for c in range(8):
                        S.op('pe', 'matmul', dict(out=banks[k][:], lhsT=GT_[:, c, s * 128:(s + 1) * 128],
                                                                           rhs=wc[:, c, nh * 512:(nh + 1) * 512], start=(c == 0), stop=(c == 7)),
                             reads=[('GT', c, s // 4), 'wreg'], writes=[('ps', k)])
                    S.op('dve', 'tensor_tensor', dict(out=yo[ys][:, nh * 512:(nh + 1) * 512], in0=banks[k][:],
                                                                                  in1=xr[xs][:, nh * 512:(nh + 1) * 512], op=ALU.add),
                         reads=[('ps', k), ('xr', xs)], writes=[('yo', ys, nh)])
                S.op('pool', 'dma_start', dict(out=dst[r0:r0 + 128, :], in_=yo[ys][:]),
                     reads=[('yo', ys, 0), ('yo', ys, 1)], dma=('yo', ys))
            S.barrier()
            A.release(mP)
            A.release(mQ)
        A.release(mL)

    def gmlp_layer(j, srcs_dsts, final):
        mL = A.mark()
        win = A.alloc("win", [128, 8, 6144], BF16)
        wout = A.alloc("wout", [128, 16, 1024], BF16)
        wst = A.alloc("wst", [128, 8, 128], BF16)
        BT = A.alloc("BT", [128, 16, 128], F32)
        for c in range(8):
            load_w(win[:, c, :], wGin[j][:, c * 6144:(c + 1) * 6144], ('win', c), 'wreg')
        load_w(wout[:], wGout[j][:, :].rearrange("p (c n) -> p c n", c=16), 'wout', 'wreg')
        load_w(wst[:], wsT[j][:, :].rearrange("p (c n) -> p c n", c=8), 'wst', 'wreg')
        load_w(BT[:], BTd[j][:, :].rearrange("p (c n) -> p c n", c=16), 'BT', 'wreg')
        S.barrier()
        wreads = [('win', c) for c in range(8)]
        fb = alloc_front(GT, nx=2)
        NS = GT // 128
        gv = A.alloc("gv", [128, 2048], F32)
        vn = A.alloc("vn", [128, NS, 2048], BF16)
        stt = [A.alloc("stt", [128, 8], F32) for _ in range(2)]
        gu = [A.alloc("gu", [128, GT], BF16) for _ in range(2)]
        sg = [A.alloc("sg", [128, GT], BF16) for _ in range(2)]
        up = [A.alloc("up", [128, GT], BF16) for _ in range(2)]
        tz = [A.alloc("tz", [128, GT], F32) for _ in range(2)]
        zT = A.alloc("zT", [128, 16, GT], BF16)
        xr = [A.alloc("xr", [128, D], F32) for _ in range(2)]
        yo = xr
        fss = [A.alloc("fss", [128, 4], F32) for _ in range(3)]
        lng = vecs[:, VLNG + j * 16:VLNG + (j + 1) * 16]
        b.gi = 0
        b.gs = 0
        for (n_tok, src, dst) in srcs_dsts:
            for t in range(n_tok // GT):
                tok0 = t * GT
                ti = b.gi
                b.gi += 1
                hT = fb.hT[ti % 2]
                hres = ('hT', ti % 2)
                front(fb, src, tok0, GT, hT, hres)
                hreads = [hres + (s,) for s in range(NS)]
                for s in range(NS):
                    if cfg.get('dbg') == 'G0':
                        continue
                    si = b.gs
                    b.gs += 1
                    stt_ = stt[si % 2]
                    kv4 = []
                    for n4 in range(4):
                        k = getbank()
                        kv4.append(k)
                        for c in range(8):
                            S.op('pe', 'matmul', dict(
                                out=banks[k][:], lhsT=hT[:, c, s * 128:(s + 1) * 128], rhs=win[:, c, 2048 + n4 * 512:2048 + (n4 + 1) * 512],
                                start=(c == 0), stop=(c == 7)), reads=[hres + (s,)] + wreads, writes=[('ps', k)])
                        S.op('act', 'activation', dict(out=gv[:, n4 * 512:(n4 + 1) * 512], in_=banks[k][:], func=AF.Gelu,
                                                                                 accum_out=stt_[:, n4:n4 + 1]),
                             reads=[('ps', k)], writes=[('gv', n4), ('stt', si % 2, n4)])
                    if cfg.get('dbg') == 'G1a':
                        continue
                    S.op('act', 'activation', dict(out=vn[:, s, :], in_=gv[:], func=AF.Square, accum_out=stt_[:, 4:5]),
                         reads=[('gv', n4) for n4 in range(4)], writes=[('vn', s), ('stt', si % 2, 4)])
                    stall = [('stt', si % 2, q) for q in range(8)]
                    if cfg.get('dbg') == 'G1b':
                        continue
                    S.op('dve', 'tensor_reduce', dict(out=stt_[:, 5:6], in_=stt_[:, 0:4], axis=mybir.AxisListType.X, op=ALU.add),
                         reads=stall[0:4], writes=[stall[5]])
                    S.op('dve', 'tensor_scalar', dict(out=stt_[:, 5:6], in0=stt_[:, 5:6], scalar1=1.0 / 2048, scalar2=None, op0=ALU.mult),
                         reads=[stall[5]], writes=[stall[5]])
                    S.op('dve', 'tensor_tensor', dict(out=stt_[:, 6:7], in0=stt_[:, 5:6], in1=stt_[:, 5:6], op=ALU.mult),
                         reads=[stall[5]], writes=[stall[6]])
                    S.op('dve', 'scalar_tensor_tensor', dict(out=stt_[:, 6:7], in0=stt_[:, 4:5], scalar=1.0 / 2048, in1=stt_[:, 6:7],
                                                                          op0=ALU.mult, op1=ALU.subtract),
                         reads=[stall[4], stall[6]], writes=[stall[6]])
                    if cfg.get('dbg') == 'G1c':
                        continue
                    S.op('act', 'activation', dict(out=stt_[:, 7:8], in_=stt_[:, 6:7], func=AF.Ln, bias=eps5[:, 0:1], scale=1.0),
                         reads=[stall[6], 'eps5'], writes=[stall[7]])
                    S.op('act', 'activation', dict(out=stt_[:, 7:8], in_=stt_[:, 7:8], func=AF.Exp, scale=-0.5),
                         reads=[stall[7]], writes=[stall[7]])
                    if cfg.get('dbg') == 'G1d':
                        continue
                    S.op('dve', 'tensor_tensor', dict(out=stt_[:, 6:7], in0=stt_[:, 5:6], in1=stt_[:, 7:8], op=ALU.mult),
                         reads=[stall[5], stall[7]], writes=[stall[6]])
                    S.op('dve', 'tensor_scalar', dict(out=stt_[:, 6:7], in0=stt_[:, 6:7], scalar1=-1.0, scalar2=None, op0=ALU.mult),
                         reads=[stall[6]], writes=[stall[6]])
                    S.op('act', 'activation', dict(out=vn[:, s, :], in_=gv[:], func=AF.Identity, scale=stt_[:, 7:8], bias=stt_[:, 6:7]),
                         reads=[('gv', n4) for n4 in range(4)] + [stall[6], stall[7]], writes=[('vn', s)])
                for c16 in range(16):
                    if cfg.get('dbg') in ('G0', 'G1', 'G1a', 'G1b', 'G1c', 'G1d'):
                        continue
                    pi = c16 % 2
                    g8 = c16 // 2
                    ku = getbank()
                    for c in range(8):
                        S.op('pe', 'matmul', dict(out=banks[ku][:, 0:GT], lhsT=win[:, c, c16 * 128:(c16 + 1) * 128], rhs=hT[:, c, :],
                                                                                start=(c == 0), stop=(c == 7)),
                             reads=hreads + wreads, writes=[('ps', ku)])
                    S.op('act', 'activation', dict(out=gu[pi][:], in_=banks[ku][:, 0:GT], func=AF.Gelu), reads=[('ps', ku)], writes=[('gu', pi)])
                    kg = getbank()
                    for c in range(8):
                        S.op('pe', 'matmul', dict(out=banks[kg][:, 0:GT], lhsT=win[:, c, 4096 + c16 * 128:4096 + (c16 + 1) * 128],
                                                                                rhs=hT[:, c, :], start=(c == 0), stop=(c == 7)),
                             reads=hreads + wreads, writes=[('ps', kg)])
                    S.op('act', 'activation', dict(out=sg[pi][:], in_=banks[kg][:, 0:GT], func=AF.Tanh, scale=0.5), reads=[('ps', kg)], writes=[('sg', pi)])
                    S.op('dve', 'scalar_tensor_tensor', dict(out=sg[pi][:], in0=sg[pi][:], scalar=1.0, in1=banks[kg][:, 0:GT], op0=ALU.add, op1=ALU.mult),
                         reads=[('ps', kg), ('sg', pi)], writes=[('sg', pi)])
                    S.op('pool', 'tensor_tensor', dict(out=up[pi][:], in0=gu[pi][:], in1=sg[pi][:], op=ALU.mult),
                         reads=[('gu', pi), ('sg', pi)], writes=[('up', pi)])
                    ks = getbank()
                    for s in range(NS):
                        S.op('pe', 'matmul', dict(out=banks[ks][:, s * 128:(s + 1) * 128], lhsT=vn[:, s, c16 * 128:(c16 + 1) * 128],
                                                                                rhs=wst[:, g8, :], start=True, stop=True),
                             reads=[('vn', s), 'wst'], writes=[('ps', ks)])
                    S.op('dve', 'scalar_tensor_tensor', dict(
                        out=tz[pi][:].rearrange("p (s x) -> p s x", s=NS), in0=banks[ks][:, 0:GT].rearrange("p (s x) -> p s x", s=NS),
                        scalar=lng[:, c16:c16 + 1], in1=BT[:, c16:c16 + 1, :].to_broadcast([128, NS, 128]), op0=ALU.mult, op1=ALU.add),
                        reads=[('ps', ks), 'BT', 'vecs'], writes=[('tz', pi)])
                    S.op('dve', 'tensor_tensor', dict(out=zT[:, c16, :], in0=tz[pi][:], in1=up[pi][:], op=ALU.mult),
                         reads=[('tz', pi), ('up', pi)], writes=[('zT', c16)])
                for s in range(NS):
                    if cfg.get('dbg') in ('G0', 'G1', 'G1a', 'G2', 'G1b', 'G1c', 'G1d'):
                        continue
                    r0 = tok0 + s * 128
                    oi = ti * NS + s
                    xs, ys = oi % 2, oi % 2
                    S.op('sp', 'dma_start', dict(out=xr[xs][:], in_=src[r0:r0 + 128, :]), writes=[('yo', xs, 0), ('yo', xs, 1)], dma=('xr', xs))
                    for nh in range(2):
                        k = getbank()
                        for c in range(16):
                            S.op('pe', 'matmul', dict(out=banks[k][:], lhsT=zT[:, c, s * 128:(s + 1) * 128],
                                                                               rhs=wout[:, c, nh * 512:(nh + 1) * 512], start=(c == 0), stop=(c == 15)),
                                 reads=[('zT', c), 'wout'], writes=[('ps', k)])
                        S.op('dve', 'tensor_tensor', dict(out=yo[ys][:, nh * 512:(nh + 1) * 512], in0=banks[k][:],
                                                                                      in1=xr[xs][:, nh * 512:(nh + 1) * 512], op=ALU.add),
                             reads=[('ps', k), ('yo', ys, nh)], writes=[('yo', ys, nh)])
                    yres = [('yo', ys, 0), ('yo', ys, 1)]
                    if final:
                        fs = fss[oi % 3]
                        fres = ('fss', oi % 3)
                        S.op('act', 'activation', dict(out=junk[:], in_=yo[ys][:], func=AF.Square, scale=1.0 / 32, accum_out=fs[:, 0:1]),
                             reads=yres, writes=['junk', fres])
                        S.op('act', 'activation', dict(out=fs[:, 1:2], in_=fs[:, 0:1], func=AF.Ln, bias=epsb[:]), reads=[fres, 'epsb'], writes=[fres])
                        S.op('act', 'activation', dict(out=fs[:, 2:3], in_=fs[:, 1:2], func=AF.Exp, scale=-0.5), reads=[fres], writes=[fres])
                        S.op('dve', 'scalar_tensor_tensor', dict(out=yo[ys][:], in0=yo[ys][:], scalar=fs[:, 2:3], in1=fgb[:],
                                                                                 op0=ALU.mult, op1=ALU.mult),
                             reads=yres + [fres, 'fgb'], writes=yres)
                    S.op('pool', 'dma_start', dict(out=dst[r0:r0 + 128, :], in_=yo[ys][:]), reads=yres, dma=('yo', ys))
        S.barrier()
        A.release(mL)

    for l in range(depth):
        j = l // 2
        sd = []
        for st in streams:
            src = st['x'] if l == 0 else st['scr'][(l - 1) % 2]
            dst = st['y'] if l == depth - 1 else st['scr'][l % 2]
            sd.append((st['n_own'], src, dst))
            if l % 2 == 0 and not (cfg.get('dbg') and (st is not streams[0] or cfg.get('dbg')[0] == 'G')):
                mla_layer(st, j, src, dst)
        if l % 2 == 1:
            gmlp_layer(j, sd, final=(l == depth - 1))
    b.peak = A.peak

    keys = S.finalize()
    with ExitStack() as es:
        sems = {k: es.enter_context(nc.semaphore("s%d" % i)) for i, k in enumerate(keys)}
        with nc.Block() as block:
            @block.tensor
            def _(e):
                S.emit('pe', e, sems)

            @block.scalar
            def _(e):
                S.emit('act', e, sems)

            @block.vector
            def _(e):
                S.emit('dve', e, sems)

            @block.gpsimd
            def _(e):
                S.emit('pool', e, sems, final_wait_dma=True)

            @block.sync
            def _(e):
                S.emit('sp', e, sems, final_wait_dma=True)
    return nc, S, b


def rope_table(n):
    inv = (10000.0 ** (-np.arange(0, 64, 2, dtype=np.float32) / np.float32(64))).astype(np.float32)
    ang = np.arange(n, dtype=np.float32)[:, None] * inv[None, :]
    ang = np.concatenate([ang, ang], axis=-1)
    cs = np.stack([np.cos(ang), np.sin(ang)], 0).astype(np.float32)
    cs = np.transpose(cs, (2, 0, 1))
    return np.ascontiguousarray(np.concatenate([cs, cs], axis=0))


def pack_vecs(norm_g, a_q_norm, a_kv_norm, b_ln_g, b_ln_b):
    v = np.zeros((128, 128), np.float32)

    def put(off, arr):
        L, n = arr.shape
        C = n // 128
        v[:, off:off + L * C] = np.transpose(arr.reshape(L, C, 128), (2, 0, 1)).reshape(128, L * C)

    put(0, norm_g)
    put(32, a_q_norm)
    put(38, a_kv_norm)
    put(42, b_ln_g)
    put(74, b_ln_b)
    return v


_CACHE = {}


def run(cfg, per_core_x, weights, cs_tabs):
    key = str([(s['name'], s['n_own'], s['skv'], s.get('exch')) for s in cfg['streams']]) + str(cfg.get('depth', 4))
    if key not in _CACHE:
        _CACHE[key] = build(cfg)
    nc, S, b = _CACHE[key]
    common = {
        "a_w_in": weights["a_w_in"], "a_w_q_up": weights["a_w_q_up"], "a_w_kv_up": weights["a_w_kv_up"], "a_w_out": weights["a_w_out"],
        "b_w_in": weights["b_w_in"], "b_w_out": weights["b_w_out"],
        "b_w_sT": np.ascontiguousarray(np.transpose(weights["b_w_s"], (0, 3, 1, 2)).reshape(2, 128, 1024)),
        "b_b_s": np.ascontiguousarray(weights["b_b_s"].reshape(2, 1024)),
        "vecs": pack_vecs(weights["norm_g"], weights["a_q_norm"], weights["a_kv_norm"], weights["b_ln_g"], weights["b_ln_b"]),
        "final_g": weights["final_g"], "ident": np.eye(128, dtype=np.float32),
    }
    in_maps = []
    ncores = len(per_core_x)
    for c in range(ncores):
        m = dict(common)
        for st in cfg['streams']:
            m["x_" + st['name']] = per_core_x[c][st['name']]
            m["cs_" + st['name']] = cs_tabs[c][st['name']]
        in_maps.append(m)
    res = run_bass_kernel_spmd(nc, in_maps, core_ids=list(range(ncores)))
    return res.results


def kernel(x_prompt, x_sample, norm_g, final_g, a_w_in, a_q_norm, a_kv_norm, a_w_q_up, a_w_kv_up, a_w_out,
           b_w_in, b_ln_g, b_ln_b, b_w_s, b_b_s, b_w_out):
    f = lambda a: np.ascontiguousarray(np.asarray(a, dtype=np.float32))
    weights = dict(norm_g=f(norm_g), final_g=f(final_g), a_w_in=f(a_w_in), a_q_norm=f(a_q_norm), a_kv_norm=f(a_kv_norm),
                   a_w_q_up=f(a_w_q_up), a_w_kv_up=f(a_w_kv_up), a_w_out=f(a_w_out), b_w_in=f(b_w_in), b_ln_g=f(b_ln_g),
                   b_ln_b=f(b_ln_b), b_w_s=f(b_w_s), b_b_s=f(b_b_s), b_w_out=f(b_w_out))
    xp, xs = f(x_prompt), f(x_sample)
    SP, SS = xp.shape[1], xs.shape[1]
    cfg = dict(streams=[dict(name="p", n_own=SP, skv=SP), dict(name="s", n_own=SS, skv=SS)], depth=4)
    tp, ts = rope_table(SP), rope_table(SS)
    per_core_x = [{"p": xp[c], "s": xs[c // 4]} for c in range(NCORES)]
    cs_tabs = [{"p": tp, "s": ts} for c in range(NCORES)]
    res = run(cfg, per_core_x, weights, cs_tabs)
    yp = np.stack([res[c]["y_p"] for c in range(NCORES)], 0)
    q = SS // 4
    ys = np.stack([np.concatenate([res[4 * s + i]["y_s"][i * q:(i + 1) * q] for i in range(4)], 0) for s in range(2)], 0)
    return yp.astype(np.float32), ys.astype(np.float32)
```
